# Optimizing a Trainium2 kernel written in Bass

```python
import jax, jax.numpy as jnp
from jax import lax
import numpy as np

D_MODEL = 1024
BATCH = 4
SEQ = 4096
DEPTH = 1

N_META = 16
ATT_HEADS = 16
HEAD_DIM = 64
ATT_WIDTH = ATT_HEADS * HEAD_DIM
POOL_WINDOWS = (2, 4, 8, 16)
POOL_GROUPS = len(POOL_WINDOWS)
POOL_WIDTH = D_MODEL
POOL_GROUP_WIDTH = POOL_WIDTH // POOL_GROUPS
MIX_WIDTH = ATT_WIDTH + POOL_WIDTH
Q_BLOCK = 128
LN_EPS = 1e-5
DEEPNORM_ALPHA = (2.0 * DEPTH) ** 0.25
DEEPNORM_BETA = (8.0 * DEPTH) ** -0.25
SPLITS = (ATT_WIDTH, 2 * ATT_WIDTH, 3 * ATT_WIDTH, 3 * ATT_WIDTH + ATT_HEADS,
          4 * ATT_WIDTH + ATT_HEADS, 4 * ATT_WIDTH + ATT_HEADS + POOL_WIDTH)
IN_COLS = 4 * ATT_WIDTH + ATT_HEADS + 2 * POOL_WIDTH

kernel_name = 'hybrid_fox_pool_deepnorm_layer'


def layer_norm(x, g, b):
    x32 = x.astype(jnp.float32)
    mu = jnp.mean(x32, axis=-1, keepdims=True)
    var = jnp.mean(jnp.square(x32 - mu), axis=-1, keepdims=True)
    y = (x32 - mu) * lax.rsqrt(var + LN_EPS) * g.astype(jnp.float32) + b.astype(jnp.float32)
    return y.astype(x.dtype)


def forgetting_attention(q, k, v, f_logit, b_forget):
    B, L, _ = q.shape
    q = q.reshape(B, L, ATT_HEADS, HEAD_DIM)
    k = k.reshape(B, L, ATT_HEADS, HEAD_DIM)
    v = v.reshape(B, L, ATT_HEADS, HEAD_DIM)
    log_f = jax.nn.log_sigmoid(f_logit.astype(jnp.float32) + b_forget.astype(jnp.float32))
    c = jnp.cumsum(log_f, axis=1)
    c_k = jnp.transpose(c, (0, 2, 1))
    key_pos = jnp.arange(L)
    scale = HEAD_DIM ** -0.5

    def attend(q_blk, c_blk, q_pos):
        s = jnp.einsum('bqhd,bkhd->bhqk', q_blk, k).astype(jnp.float32) * scale
        s = s + jnp.transpose(c_blk, (0, 2, 1))[..., :, None] - c_k[:, :, None, :]
        mask = key_pos[None, :] <= q_pos[:, None]
        s = jnp.where(mask[None, None], s, -jnp.inf)
        p = jax.nn.softmax(s, axis=-1).astype(v.dtype)
        return jnp.einsum('bhqk,bkhd->bqhd', p, v)

    o_meta = attend(q[:, :N_META], c[:, :N_META], jnp.arange(N_META))
    n_blk = (L - N_META) // Q_BLOCK
    q_blocks = jnp.moveaxis(q[:, N_META:].reshape(B, n_blk, Q_BLOCK, ATT_HEADS, HEAD_DIM), 1, 0)
    c_blocks = jnp.moveaxis(c[:, N_META:].reshape(B, n_blk, Q_BLOCK, ATT_HEADS), 1, 0)
    pos_blocks = (N_META + jnp.arange(n_blk * Q_BLOCK)).reshape(n_blk, Q_BLOCK)
    o_real = lax.map(lambda args: attend(*args), (q_blocks, c_blocks, pos_blocks))
    o_real = jnp.moveaxis(o_real, 0, 1).reshape(B, n_blk * Q_BLOCK, ATT_HEADS, HEAD_DIM)
    o = jnp.concatenate([o_meta, o_real], axis=1)
    return o.reshape(B, L, ATT_WIDTH)


def multiscale_pool(u, w_pool, pool_scale):
    B, L, _ = u.shape
    u32 = u.astype(jnp.float32)
    cs = jnp.pad(jnp.cumsum(u32, axis=1), ((0, 0), (1, 0), (0, 0)))
    t = jnp.arange(L)
    outs = []
    for gi, w in enumerate(POOL_WINDOWS):
        sl = slice(gi * POOL_GROUP_WIDTH, (gi + 1) * POOL_GROUP_WIDTH)
        csg = cs[:, :, sl]
        hi = csg[:, 1:]
        lo = jnp.pad(csg[:, :L + 1 - w], ((0, 0), (w - 1, 0), (0, 0)))
        cnt = jnp.minimum(t + 1, w).astype(jnp.float32)[None, :, None]
        outs.append((hi - lo) / cnt - u32[:, :, sl])
    d = jnp.stack(outs, axis=2)
    y = jnp.einsum('blgc,gce->blge', d, w_pool.astype(jnp.float32))
    y = y.reshape(B, L, POOL_WIDTH) * pool_scale.astype(jnp.float32)
    return y.astype(u.dtype)


def hybrid_layer(x, w_in, b_forget, w_pool, pool_scale, w_out, ln_g, ln_b):
    h = jnp.einsum('bld,dc->blc', x, w_in)
    q, k, v, f_logit, g_att, u, g_pool = jnp.split(h, SPLITS, axis=-1)
    a = forgetting_attention(q, k, v, f_logit, b_forget) * jax.nn.silu(g_att)
    p = multiscale_pool(u, w_pool, pool_scale) * jax.nn.silu(g_pool)
    y = jnp.einsum('blc,cd->bld', jnp.concatenate([a, p], axis=-1), w_out)
    return layer_norm(DEEPNORM_ALPHA * x + y, ln_g, ln_b)


def setup_inputs(seed: int = 0) -> dict:
    key = jax.random.key(seed)
    ks = jax.random.split(key, 13)
    f32 = jnp.float32
    x = jax.random.normal(ks[0], (BATCH, SEQ, D_MODEL), f32)
    meta_tokens = jax.random.normal(ks[1], (N_META, D_MODEL), f32)
    ln_in_g = 1.0 + 0.02 * jax.random.normal(ks[2], (D_MODEL,), f32)
    ln_in_b = 0.02 * jax.random.normal(ks[3], (D_MODEL,), f32)
    w_in = jax.random.normal(ks[4], (DEPTH, D_MODEL, IN_COLS), f32) * D_MODEL ** -0.5
    b_forget = (jnp.linspace(1.0, 6.0, ATT_HEADS, dtype=f32)[None, :]
                + 0.1 * jax.random.normal(ks[5], (DEPTH, ATT_HEADS), f32))
    w_pool = jax.random.normal(ks[6], (DEPTH, POOL_GROUPS, POOL_GROUP_WIDTH, POOL_GROUP_WIDTH), f32) * POOL_GROUP_WIDTH ** -0.5
    pool_scale = 1.0 + 0.1 * jax.random.normal(ks[7], (DEPTH, POOL_WIDTH), f32)
    w_out = jax.random.normal(ks[8], (DEPTH, MIX_WIDTH, D_MODEL), f32) * (MIX_WIDTH ** -0.5 * DEEPNORM_BETA)
    ln_g = 1.0 + 0.02 * jax.random.normal(ks[9], (DEPTH, D_MODEL), f32)
    ln_b = 0.02 * jax.random.normal(ks[10], (DEPTH, D_MODEL), f32)
    return {'x': x, 'meta_tokens': meta_tokens, 'ln_in_g': ln_in_g, 'ln_in_b': ln_in_b,
            'w_in': w_in, 'b_forget': b_forget, 'w_pool': w_pool, 'pool_scale': pool_scale,
            'w_out': w_out, 'ln_g': ln_g, 'ln_b': ln_b}


def reference(x, meta_tokens, ln_in_g, ln_in_b, w_in, b_forget, w_pool, pool_scale, w_out, ln_g, ln_b):
    B = x.shape[0]
    meta = jnp.broadcast_to(meta_tokens[None].astype(x.dtype), (B, N_META, D_MODEL))
    h = jnp.concatenate([meta, x], axis=1)
    h = layer_norm(h, ln_in_g, ln_in_b)
    for layer in range(DEPTH):
        h = hybrid_layer(h, w_in[layer], b_forget[layer], w_pool[layer], pool_scale[layer],
                         w_out[layer], ln_g[layer], ln_b[layer])
    return h[:, N_META:]
```

```python
import numpy as np
from contextlib import ExitStack
import concourse.bass as bass
import concourse.mybir as mybir
from concourse.bass_utils import run_bass_kernel_spmd

F32 = mybir.dt.float32
BF16 = mybir.dt.bfloat16
AF = mybir.ActivationFunctionType
ALU = mybir.AluOpType

D = 1024
SEQ = 4096
NM = 16
L = SEQ + NM
H = 16
NOWN = 2048
NHALO = 256
ROWS = NM + SEQ + NHALO
INCOLS = 6160
C_Q, C_K, C_V, C_F, C_G, C_U, C_GP = 0, 1024, 2048, 3072, 3088, 4112, 5136
ALPHA = 2.0 ** 0.25
EPS = 1e-5
MASKV = -30000.0
LA = 2


class Buf:
    __slots__ = ("w", "r")

    def __init__(self):
        self.w = {}
        self.r = {}


class Sched:
    ENG = ("pe", "act", "dve", "pool", "sp")

    def __init__(self):
        self.q = {n: [] for n in self.ENG}
        self.dcount = {}
        self.bufs = []
        self.cps = []

    def buf(self):
        b = Buf()
        self.bufs.append(b)
        return b

    def bufs_n(self, n):
        return [self.buf() for _ in range(n)]

    def _deps(self, eng, reads, writes, writes_nc):
        waits = {}

        def add(k, v, war=False):
            if k[0] == "E" and k[1] == eng and eng == "pe":
                return
            if waits.get(k, -1) < v:
                waits[k] = v

        for b in reads:
            for k, v in b.w.items():
                add(k, v)
        for b in writes:
            for k, v in b.w.items():
                add(k, v)
            for k, v in b.r.items():
                add(k, v, war=True)
        for b in writes_nc:
            for k, v in b.r.items():
                add(k, v, war=True)
        return waits

    @staticmethod
    def _mark(tok, reads, writes, writes_nc):
        k = (tok[0], tok[1])
        for b in reads:
            if b.r.get(k, -1) < tok[2]:
                b.r[k] = tok[2]
        for b in writes:
            b.w = {k: tok[2]}
            b.r = {}
        for b in writes_nc:
            if b.w.get(k, -1) < tok[2]:
                b.w[k] = tok[2]

    def op(self, eng, fn, reads=(), writes=(), writes_nc=(), sig=True):
        waits = self._deps(eng, reads, writes, writes_nc)
        tok = ("E", eng, len(self.q[eng]))
        self.q[eng].append(dict(kind="op", fn=fn, waits=waits, sig=sig))
        self._mark(tok, reads, writes, writes_nc)
        return tok

    def dma(self, eng, slot, out, in_, reads=(), writes=(), writes_nc=(), **kw):
        waits = self._deps(eng, reads, writes, writes_nc)
        self.dcount[slot] = self.dcount.get(slot, 0) + 16
        tok = ("D", slot, self.dcount[slot])
        self.q[eng].append(dict(kind="dma", out=out, in_=in_, waits=waits, slot=slot, sig=False, kw=kw))
        self._mark(tok, reads, writes, writes_nc)
        return tok

    def barrier(self, skip_slots=(), keep_bufs=()):
        toks = []
        for e in self.ENG:
            ops = self.q[e]
            last = None
            for i in range(len(ops) - 1, -1, -1):
                if ops[i]["kind"] == "op":
                    last = i
                    break
            if last is not None:
                ops[last]["sig"] = True
                toks.append(("E", e, last))
        for slot, v in self.dcount.items():
            if slot not in skip_slots:
                toks.append(("D", slot, v))
        for e in self.ENG:
            waits = {}
            for t in toks:
                if t[0] == "E" and t[1] == e:
                    continue
                waits[(t[0], t[1])] = t[2]
            self.q[e].append(dict(kind="wait", waits=waits, sig=False))
        for b in self.bufs:
            if any(b is kb for kb in keep_bufs):
                continue
            b.w = {}
            b.r = {}
        self.cps.append({e: len(self.q[e]) for e in self.ENG})

    def replay(self, nc, block, sems, upto=None):
        if upto is not None:
            for e in self.ENG:
                self.q[e] = self.q[e][:self.cps[upto][e]]
        vals = {}
        for e, ops in self.q.items():
            cnt = 0
            for o in ops:
                if o["kind"] == "op" and o["sig"]:
                    cnt += 1
                o["cum"] = cnt
            nxt = None
            v = [None] * len(ops)
            for i in range(len(ops) - 1, -1, -1):
                if ops[i]["kind"] == "op" and ops[i]["sig"]:
                    nxt = ops[i]["cum"]
                v[i] = nxt
            vals[e] = v

        def run(e, handle):
            waited = {}
            own = sems[("E", e)]
            for o in self.q[e]:
                for key, val in o["waits"].items():
                    if key[0] == "E":
                        val = vals[key[1]][val]
                        assert val is not None
                    if waited.get(key, -1) >= val:
                        continue
                    waited[key] = val
                    handle.wait_ge(sems[key], val)
                if o["kind"] == "op":
                    ins = o["fn"](handle)
                    if o["sig"]:
                        ins.then_inc(own, 1)
                elif o["kind"] == "dma":
                    handle.dma_start(out=o["out"], in_=o["in_"], **o["kw"]).then_inc(sems[("D", o["slot"])], 16)

        block.tensor(lambda h: run("pe", h))
        block.scalar(lambda h: run("act", h))
        block.vector(lambda h: run("dve", h))
        block.gpsimd(lambda h: run("pool", h))
        block.sync(lambda h: run("sp", h))


def build_nc(upto=None):
    nc = bass.Bass("TRN2", target_bir_lowering=False)
    S = Sched()

    def din(name, shape):
        return nc.dram_tensor(name, shape, F32, kind="ExternalInput").ap()

    xs = din("xs", [ROWS, D])
    w_in = din("w_in", [D, INCOLS])
    w_pool = din("w_pool", [4, 256, 256])
    w_out = din("w_out", [2048, D])
    cvec_d = din("cvec", [128, 24])
    bfv_d = din("bfv", [16, 1])
    mmat_d = din("mmat", [32, 32])
    ident_d = din("ident", [128, 128])
    masks_d = din("masks", [128, 256])
    lnrows_d = din("lnrows", [4, D])
    y = nc.dram_tensor("y", [NOWN, D], F32, kind="ExternalOutput").ap()
    wo_bf = nc.dram_tensor("wo_bf", [2048, D], BF16, kind="Internal").ap()

    DUMP = False

    def dump(name, ap, shape, dt):
        if not DUMP:
            return
        o = nc.dram_tensor("dbg_" + name, list(shape), dt, kind="ExternalOutput").ap()
        S.dma("sp", "dbg", o, ap)
        S.barrier()
        S.cps.pop()

    es = ExitStack()

    def sb(stack, name, shape, dt, side=None):
        if side is None:
            return stack.enter_context(nc.sbuf_tensor(name, shape, dt))
        return stack.enter_context(nc.sbuf_tensor(name, shape, dt, side=side))

    ps = [es.enter_context(nc.psum_tensor(f"ps{i}", [128, 512], F32)) for i in range(8)]
    psb = S.bufs_n(8)

    mixA = sb(es, "mixA", [128, 8, NOWN], BF16)
    ident_f = sb(es, "ident_f", [128, 128], F32)
    ident_b = sb(es, "ident_b", [128, 128], BF16)
    masks_b = sb(es, "masks_b", [128, 256], BF16)
    cv = sb(es, "cv", [128, 24], F32)
    epsT = sb(es, "epsT", [128, 1], F32)
    oneT = sb(es, "oneT", [128, 1], F32)
    nbf = sb(es, "nbf", [16, 1], F32)
    mmat = sb(es, "mmat_s", [32, 32], F32)
    negc_tok = sb(es, "negc_tok", [128, 33, 16], F32)
    sig8r = sb(es, "sig8r", [128, 256], BF16)
    constb = S.buf()
    wq = sb(es, "wqkvg", [128, 4, 8, 128], BF16)
    NT0 = 35
    mv_all = sb(es, "mv_all", [128, NT0, 2], F32)
    sd_all = sb(es, "sd_all", [128, NT0], F32)
    rs_all = sb(es, "rs_all", [128, NT0], F32)
    s1_all = sb(es, "s1_all", [128, NT0], F32)
    s2_all = sb(es, "s2_all", [128, NT0], F32)
    t1_all = sb(es, "t1_all", [128, NT0], F32)

    S.dma("sp", "const", ident_f[:], ident_d[:, :], writes=[])
    S.dma("sp", "const", cv[:], cvec_d[:, :], writes=[])
    S.dma("sp", "const", nbf[:], bfv_d[:, :], writes=[])
    S.dma("sp", "const", mmat[:], mmat_d[:, :], writes=[])
    S.dma("pool", "constp", ident_b[:], ident_d[:, :], writes=[])
    S.dma("pool", "constp", masks_b[:], masks_d[:, :], writes=[])
    S.op("dve", lambda e: e.memset(negc_tok[:, 0, :], MASKV))
    S.op("dve", lambda e: e.memset(epsT[:], EPS))
    S.op("dve", lambda e: e.memset(oneT[:], 1.0))
    S.barrier()
    S.op("dve", lambda e: e.tensor_scalar(out=nbf[:], in0=nbf[:], scalar1=-1.0, scalar2=None, op0=ALU.mult))

    es_x = ExitStack()
    xnT = sb(es_x, "xnT", [128, 8, L], BF16, side="right")
    xnh = sb(es_x, "xnh", [128, 8, NHALO], BF16, side="right")

    for gi_w, cbase_w in enumerate((C_Q, C_K, C_V, C_G)):
        S.dma("pool", "wq", wq[:, gi_w, :, :],
              w_in[:, cbase_w:cbase_w + 128].rearrange("(c p) n -> p c n", p=128))

    with ExitStack() as p0:
        NXT = 6
        NXH = 8
        xt = [sb(p0, f"xt{i}", [128, D], F32) for i in range(NXT)]
        xtb = S.bufs_n(NXT)
        xh = [sb(p0, f"xh{i}", [128, D], F32) for i in range(NXH)]
        xhb = S.bufs_n(NXH)
        st = [sb(p0, f"st{i}", [128, 2, 6], F32) for i in range(NXT)]
        stb = S.bufs_n(NXT)
        mvb, sdb, rsb = S.bufs_n(NT0), S.bufs_n(NT0), S.bufs_n(NT0)

        groups = [[(0, NM, xnT, 0)]]
        for g4 in range(8):
            groups.append([(NM + 128 * s, 128, xnT, NM + 128 * s) for s in range(4 * g4, 4 * g4 + 4)])
        groups.append([(NM + SEQ, 128, xnh, 0), (NM + SEQ + 128, 128, xnh, 128)])
        flat = []
        for gi_, grp in enumerate(groups):
            for j_, (r0, P, dst, c0) in enumerate(grp):
                flat.append((gi_, j_, r0, P))
        bcount = [0]

        junk2 = [sb(p0, f"junk{i}", [128, D], BF16) for i in range(2)]
        junkb2 = S.bufs_n(2)
        s1b, s2b, t1b = S.bufs_n(NT0), S.bufs_n(NT0), S.bufs_n(NT0)

        def stageA1(ti):
            gi_, j_, r0, P = flat[ti]
            k = ti % NXT
            tt_ = ti
            S.dma("sp", f"xt{k}", xt[k][0:P, :], xs[r0:r0 + P, :], writes=[xtb[k]])
            S.op("act", lambda e: e.activation(out=junk2[0][0:P, :], in_=xt[k][0:P, :], func=AF.Identity,
                                               accum_out=s1_all[0:P, tt_:tt_ + 1]),
                 reads=[xtb[k]], writes=[s1b[tt_], junkb2[0]])
            S.op("act", lambda e: e.activation(out=junk2[1][0:P, :], in_=xt[k][0:P, :], func=AF.Square,
                                               accum_out=s2_all[0:P, tt_:tt_ + 1]),
                 reads=[xtb[k]], writes=[s2b[tt_], junkb2[1]])

        def stageA2(ti):
            gi_, j_, r0, P = flat[ti]
            tt_ = ti
            S.op("dve", lambda e: e.tensor_scalar(out=mv_all[0:P, tt_, 0:1], in0=s1_all[0:P, tt_:tt_ + 1], scalar1=1.0 / D,
                                                  scalar2=None, op0=ALU.mult), reads=[s1b[tt_]], writes=[mvb[tt_]])
            S.op("dve", lambda e: e.tensor_tensor(out=t1_all[0:P, tt_:tt_ + 1], in0=mv_all[0:P, tt_, 0:1], in1=mv_all[0:P, tt_, 0:1],
                                                  op=ALU.mult), reads=[mvb[tt_]], writes=[t1b[tt_]])
            S.op("dve", lambda e: e.scalar_tensor_tensor(out=mv_all[0:P, tt_, 1:2], in0=s2_all[0:P, tt_:tt_ + 1], scalar=1.0 / D,
                                                         in1=t1_all[0:P, tt_:tt_ + 1], op0=ALU.mult, op1=ALU.subtract),
                 reads=[s2b[tt_], t1b[tt_]], writes_nc=[mvb[tt_]])
            S.op("act", lambda e: e.activation(out=sd_all[0:P, tt_:tt_ + 1], in_=mv_all[0:P, tt_, 1:2], func=AF.Sqrt,
                                               bias=epsT[0:P, 0:1], scale=1.0),
                 reads=[mvb[tt_]], writes=[sdb[tt_]])

        def stageB(ti):
            gi_, j_, r0, P = flat[ti]
            k = ti % NXT
            kh = ti % NXH
            tt_ = ti
            S.op("dve", lambda e: e.reciprocal(out=rs_all[0:P, tt_:tt_ + 1], in_=sd_all[0:P, tt_:tt_ + 1]),
                 reads=[sdb[tt_]], writes=[rsb[tt_]])
            S.op("dve", lambda e: e.tensor_scalar(
                out=xh[kh][0:P, :], in0=xt[k][0:P, :], scalar1=mv_all[0:P, tt_, 0:1], scalar2=rs_all[0:P, tt_:tt_ + 1],
                op0=ALU.subtract, op1=ALU.mult),
                reads=[xtb[k], mvb[tt_], rsb[tt_]], writes=[xhb[kh]])

        def stageC(gi_, ti_last):
            grp = groups[gi_]
            ti0 = ti_last - len(grp) + 1
            dst, c00 = grp[0][2], grp[0][3]
            ncols = sum(P for (_, P, _, _) in grp)
            for c in range(8):
                bank = bcount[0] % 4
                bcount[0] += 1
                off = 0
                for j, (r0, P, _, _) in enumerate(grp):
                    kh = (ti0 + j) % NXH
                    S.op("pe", lambda e, bank=bank, off=off, c=c, kh=kh, P=P: e.transpose(
                        out=ps[bank][:, off:off + P], in_=xh[kh][0:P, c * 128:(c + 1) * 128],
                        identity=ident_f[0:P, 0:P]),
                        reads=[xhb[kh]], writes=[psb[bank]], sig=(j == len(grp) - 1))
                    off += P
                S.op("dve", lambda e, bank=bank, c=c, dst=dst, c00=c00, ncols=ncols: e.tensor_scalar(
                    out=dst[:, c, c00:c00 + ncols], in0=ps[bank][:, 0:ncols],
                    scalar1=cv[:, c:c + 1], scalar2=cv[:, 8 + c:9 + c], op0=ALU.mult, op1=ALU.add),
                    reads=[psb[bank]], writes=[])

        nflat = len(flat)
        stageA1(0)
        stageA1(1)
        stageA1(2)
        stageA2(0)
        for ti in range(nflat):
            if ti + 3 < nflat:
                stageA1(ti + 3)
            if ti + 1 < nflat:
                stageA2(ti + 1)
            stageB(ti)
            gi_, j_, _, _ = flat[ti]
            if j_ == len(groups[gi_]) - 1:
                stageC(gi_, ti)
        S.barrier()

    with ExitStack() as pa:
        wkqvb, wgb = S.bufs_n(2)
        KA = [sb(pa, "KA0", [65, L], BF16), None]
        KB = [sb(pa, "KB0", [65, L], BF16), None]
        QA = [sb(pa, "QA0", [65, NOWN], BF16), None]
        QB = [sb(pa, "QB0", [65, NOWN], BF16), None]
        V1 = [sb(pa, "V1_0", [128, 33, 192], BF16), None]
        sg = [sb(pa, "sg0", [128, NOWN], BF16), None]
        sig8rb = S.buf()
        Kb, Qb, Vb, sgb = S.bufs_n(2), S.bufs_n(2), S.bufs_n(2), S.bufs_n(2)
        aTb = S.buf()
        NP = 3
        Pt = [sb(pa, f"Pt{i}", [128, 512], BF16) for i in range(NP)]
        Ptb = S.bufs_n(NP)
        recb = sb(pa, "recb0", [128, 512], F32)
        recbb = S.buf()
        tt = sb(pa, "tt0", [128, 512], F32)
        ttb = S.buf()

        Vzb = S.bufs_n(2)

        def memset_aug(i2):
            S.op("pool", lambda e: e.memset(KA[i2][64:65, :], 1.0), writes_nc=[Kb[i2]])
            S.op("pool", lambda e: e.memset(KB[i2][64:65, :], 1.0), writes_nc=[Kb[i2]])
            S.op("pool", lambda e: e.memset(V1[i2][:, :, 64:128], 1.0), writes_nc=[Vb[i2]])
            S.op("pool", lambda e: e.memset(V1[i2][:, 0, 0:64], 0.0), writes_nc=[Vb[i2], Vzb[i2]])
            S.op("pool", lambda e: e.memset(V1[i2][:, 0, 128:192], 0.0), writes_nc=[Vb[i2], Vzb[i2]])
        memset_aug(0)

        def load_kqv(hp):
            for gi, cbase in enumerate((C_Q, C_K, C_V)):
                S.dma("pool", "wq", wq[:, gi, :, :],
                      w_in[:, cbase + 128 * hp:cbase + 128 * (hp + 1)].rearrange("(c p) n -> p c n", p=128),
                      writes_nc=[wkqvb])

        def load_g(hp):
            S.dma("pool", "wg", wq[:, 3, :, :],
                  w_in[:, C_G + 128 * hp:C_G + 128 * (hp + 1)].rearrange("(c p) n -> p c n", p=128),
                  writes_nc=[wgb])

        SB = [2, 3, 4]
        OB = [5, 6]
        PB = [0, 1, 7]
        LA_ = 2
        pcount = [0]
        Wpair = wq

        def proj_bank():
            bk = PB[pcount[0] % len(PB)]
            pcount[0] += 1
            return bk

        def mm8(gi, rhs_of, N, state, lo_c, hi_c):
            bank = state["bank"]
            for c in range(lo_c, hi_c):
                S.op("pe", lambda e, bank=bank, c=c: e.matmul(
                    ps[bank][:, 0:N], lhsT=Wpair[:, gi, c, :], rhs=rhs_of(c), start=(c == 0), stop=(c == 7)),
                    reads=[wkqvb if gi < 3 else wgb], writes=[psb[bank]], sig=(c == 7))

        def proj_chunks(hp):
            k = hp % 2
            out = []

            def sig_dma():
                S.dma("pool", f"sigA{k}", QA[k][64:65, :].rearrange("o (a b) -> o a b", b=256), sig8r[16 * hp:16 * hp + 8, :], reads=[sig8rb], writes_nc=[Qb[k]])
                S.dma("pool", f"sigB{k}", QB[k][64:65, :].rearrange("o (a b) -> o a b", b=256), sig8r[16 * hp + 8:16 * hp + 16, :], reads=[sig8rb], writes_nc=[Qb[k]])
            out.append(sig_dma)
            jobs = []
            for t in range((L + 511) // 512):
                c0 = 512 * t
                jobs.append((1, c0, min(512, L - c0), KA[k], KB[k], c0, Kb[k]))
            for t in range(4):
                jobs.append((0, NM + 512 * t, 512, QA[k], QB[k], 512 * t, Qb[k]))
            for (gi, c0, N, TA, TB, d0, tb) in jobs:
                state = {}

                def first(gi=gi, c0=c0, N=N, state=state):
                    state["bank"] = proj_bank()
                    mm8(gi, lambda c: xnT[:, c, c0:c0 + N], N, state, 0, 4)

                def second(gi=gi, c0=c0, N=N, TA=TA, TB=TB, d0=d0, tb=tb, state=state):
                    mm8(gi, lambda c: xnT[:, c, c0:c0 + N], N, state, 4, 8)
                    bank = state["bank"]
                    S.op("dve", lambda e: e.tensor_copy(out=TA[0:64, d0:d0 + N], in_=ps[bank][0:64, 0:N]),
                         reads=[psb[bank]], writes_nc=[tb])
                    S.op("dve", lambda e: e.tensor_copy(out=TB[0:64, d0:d0 + N], in_=ps[bank][64:128, 0:N]),
                         reads=[psb[bank]], writes_nc=[tb])
                out.append(first)
                out.append(second)
            vblocks = [(0, NM, 0)] + [(NM + 128 * s, 128, 1 + s) for s in range(32)]
            for g0 in range(0, 33, 4):
                grp = vblocks[g0:g0 + 4]
                state = {}
                for j, (c0, P, slot) in enumerate(grp):
                    def vblk(j=j, c0=c0, P=P, grp=grp, g0=g0, state=state):
                        if j == 0:
                            state["bank"] = proj_bank()
                        bank = state["bank"]
                        lastj = (j == len(grp) - 1)
                        for c in range(8):
                            S.op("pe", lambda e, c=c: e.matmul(
                                ps[bank][0:P, 128 * j:128 * (j + 1)], lhsT=xnT[:, c, c0:c0 + P], rhs=Wpair[:, 2, c, :],
                                start=(c == 0), stop=(c == 7)),
                                reads=[wkqvb], writes=[psb[bank]], sig=(c == 7 and lastj))
                        if not lastj:
                            return
                        if g0 == 0:
                            S.op("dve", lambda e: e.tensor_copy(out=V1[k][0:NM, 0, 0:64], in_=ps[bank][0:NM, 0:64]),
                                 reads=[psb[bank], Vzb[k]], writes_nc=[Vb[k]])
                            S.op("dve", lambda e: e.tensor_copy(out=V1[k][0:NM, 0, 128:192], in_=ps[bank][0:NM, 64:128]),
                                 reads=[psb[bank], Vzb[k]], writes_nc=[Vb[k]])
                            lo = 1
                        else:
                            lo = 0
                        n = len(grp) - lo
                        if n > 0:
                            s0 = grp[lo][2]
                            for (vo, po_) in ((0, 0), (128, 64)):
                                S.op("dve", lambda e, vo=vo, po_=po_: e.tensor_copy(
                                    out=V1[k][:, s0:s0 + n, vo:vo + 64],
                                    in_=ps[bank][:, 128 * lo:128 * (lo + n)].rearrange("p (s c) -> p s c", c=128)[:, :, po_:po_ + 64]),
                                    reads=[psb[bank]], writes_nc=[Vb[k]])
                    out.append(vblk)
            return out

        def g_proj(hp):
            k = hp % 2
            for t in range(4):
                c0 = NM + 512 * t
                state = {"bank": proj_bank()}
                mm8(3, lambda c, c0=c0: xnT[:, c, c0:c0 + 512], 512, state, 0, 8)
                bank = state["bank"]
                S.op("act", lambda e, bank=bank, t=t, k=k: e.activation(out=sg[k][:, 512 * t:512 * (t + 1)], in_=ps[bank][:, :], func=AF.Silu),
                     reads=[psb[bank]], writes_nc=[sgb[k]])

        def attention(hp, pre, inserts, post=()):
            k = hp % 2
            tl = []
            for hh in range(2):
                for J in range(4):
                    lst = [(128, 0, 0, 0, 512, None)]
                    for i in range(4 * J):
                        lst.append((128, NM + 128 * i, 1 + i, 0, 512, None))
                    for j in range(4 * J):
                        lst.append((128, NM + NOWN + 128 * j, 17 + j, 0, 512, None))
                    for sp_ in range(4):
                        i = 4 * J + sp_
                        lst.append((128, NM + 128 * i, 1 + i, 128 * sp_, 512 - 128 * sp_, 0))
                        lst.append((128, NM + NOWN + 128 * i, 17 + i, 128 * sp_, 512 - 128 * sp_, 1))
                    for n_, it in enumerate(lst):
                        tl.append(((hh, J), it, n_ == 0, n_ == len(lst) - 1))
            nt = len(tl)
            START = 4
            post = list(post)
            ins_i = 0
            obank_of = {}
            for idx in range(nt + LA_):
                if idx < nt:
                    (hh, J), (M, kc0, slot, qoff, N, mask), first, last = tl[idx]
                    Kt = KA[k] if hh == 0 else KB[k]
                    Qt = QA[k] if hh == 0 else QB[k]
                    sbank = SB[idx % len(SB)]
                    q0 = 512 * J + qoff
                    S.op("pe", lambda e, sbank=sbank, M=M, kc0=kc0, q0=q0, N=N, Kt=Kt, Qt=Qt, mask=mask: e.matmul(
                        ps[sbank][0:M, 0:N], lhsT=Kt[0:65, kc0:kc0 + M], rhs=Qt[0:65, q0:q0 + N], start=True,
                        stop=(mask is None)),
                        reads=[Kb[k], Qb[k]], writes=[psb[sbank]], sig=(mask is None))
                    if mask is not None:
                        S.op("pe", lambda e, sbank=sbank, mask=mask: e.matmul(
                            ps[sbank][:, 0:128], lhsT=ident_b[:, :], rhs=masks_b[:, 128 * mask:128 * (mask + 1)],
                            start=False, stop=True),
                            reads=[], writes=[psb[sbank]], sig=True)
                    if idx == 0:
                        for f_ in pre:
                            f_()
                    if idx >= START and ins_i < len(inserts):
                        want = ((idx - START + 1) * len(inserts) + (nt - 4 - START) - 1) // max(1, nt - 4 - START)
                        while ins_i < min(want, len(inserts)):
                            inserts[ins_i]()
                            ins_i += 1
                        if ins_i == len(inserts):
                            while post:
                                post.pop(0)()
                j = idx - LA_
                if j >= 0:
                    (hh, J), (M, kc0, slot, qoff, N, mask), first, last = tl[j]
                    sbank = SB[j % len(SB)]
                    pk = j % NP
                    h = 2 * hp + hh
                    if first:
                        obank_of[(hh, J)] = OB[ogroup[0] % 2]
                        ogroup[0] += 1
                    ob = obank_of[(hh, J)]
                    S.op("act", lambda e, sbank=sbank, pk=pk, M=M, N=N, slot=slot, h=h: e.activation(
                        out=Pt[pk][0:M, 0:N], in_=ps[sbank][0:M, 0:N], func=AF.Exp,
                        bias=negc_tok[0:M, slot, h:h + 1], scale=0.125),
                        reads=[psb[sbank]], writes=[Ptb[pk]])
                    vc0 = 0 if hh == 0 else 64
                    S.op("pe", lambda e, ob=ob, pk=pk, M=M, N=N, slot=slot, vc0=vc0, qoff=qoff, first=first, last=last: e.matmul(
                        ps[ob][:, qoff:qoff + N], lhsT=V1[k][0:M, slot, vc0:vc0 + 128], rhs=Pt[pk][0:M, 0:N],
                        start=first, stop=last),
                        reads=[Ptb[pk], Vb[k]], writes=[psb[ob]], sig=(last or idx >= nt - 1))
                    if last:
                        dr = slice(64, 128) if hh == 0 else slice(0, 64)
                        vr = slice(0, 64) if hh == 0 else slice(64, 128)
                        S.op("dve", lambda e, ob=ob, dr=dr: e.reciprocal(out=recb[dr, :], in_=ps[ob][dr, :]),
                             reads=[psb[ob]], writes=[recbb])
                        S.op("dve", lambda e, ob=ob, dr=dr, vr=vr: e.tensor_tensor(
                            out=tt[vr, :], in0=ps[ob][vr, :], in1=recb[dr, :], op=ALU.mult),
                            reads=[psb[ob], recbb], writes=[ttb])
                        S.op("dve", lambda e, vr=vr, J=J, hp=hp: e.tensor_tensor(
                            out=mixA[vr, hp, 512 * J:512 * (J + 1)], in0=tt[vr, :], in1=sg[k][vr, 512 * J:512 * (J + 1)],
                            op=ALU.mult),
                            reads=[ttb, sgb[k]], writes_nc=[aTb])
            while ins_i < len(inserts):
                inserts[ins_i]()
                ins_i += 1
            while post:
                post.pop(0)()

        ogroup = [0]
        NPAIR = 8
        with ExitStack() as p1:
            wf = sb(p1, "wf", [128, 8, 16], BF16)
            lsp = sb(p1, "lsp", [16, L], F32)
            Sc = sb(p1, "Sc", [16, L], F32)
            et = [sb(p1, f"et{i}", [16, 512], F32) for i in range(2)]
            etb = S.bufs_n(2)
            Tt = sb(p1, "Tt", [16, 32], F32)
            TT = sb(p1, "TT", [32, 16], F32)
            Dsb = sb(p1, "Dsb", [16, 32], F32)
            wfb, lspb, Scb, Ttb, TTb, Dsbb, nctb = S.bufs_n(7)
            sig8 = sb(p1, "sig8", [16, NOWN], BF16)
            sig8b = S.buf()
            S.dma("pool", "wf", wf[:], w_in[:, C_F:C_F + 16].rearrange("(c p) n -> p c n", p=128), writes=[wfb])
            ntile = (L + 511) // 512
            for t in range(ntile):
                c0 = 512 * t
                N = min(512, L - c0)
                bank = 6 + t % 2
                k = t % 2
                for c in range(8):
                    S.op("pe", lambda e, bank=bank, c=c, c0=c0, N=N: e.matmul(
                        ps[bank][0:16, 0:N], lhsT=wf[:, c, :], rhs=xnT[:, c, c0:c0 + N], start=(c == 0), stop=(c == 7)),
                        reads=[wfb], writes=[psb[bank]], sig=(c == 7))
                S.op("act", lambda e, bank=bank, k=k, N=N: e.activation(
                    out=et[k][:, 0:N], in_=ps[bank][0:16, 0:N], func=AF.Exp, bias=nbf[:, 0:1], scale=-1.0),
                    reads=[psb[bank]], writes=[etb[k]])
                S.op("act", lambda e, k=k, c0=c0, N=N: e.activation(
                    out=lsp[:, c0:c0 + N], in_=et[k][:, 0:N], func=AF.Ln, bias=oneT[0:16, 0:1], scale=1.0),
                    reads=[etb[k]], writes_nc=[lspb])
            S.op("dve", lambda e: e.tensor_tensor_scan(
                out=Sc[:, :], data0=oneT[0:16, 0:1].to_broadcast([16, L]), data1=lsp[:, :], initial=0.0,
                op0=ALU.mult, op1=ALU.add), reads=[lspb], writes=[Scb])
            S.op("dve", lambda e: e.tensor_tensor(
                out=Tt[:, :], in0=Sc[:, 143:143 + 128 * 31 + 1:128], in1=Sc[:, 15:15 + 128 * 31 + 1:128], op=ALU.subtract),
                reads=[Scb], writes=[Ttb])
            S.op("pe", lambda e: e.transpose(out=ps[2][0:32, 0:16], in_=Tt[:, :], identity=ident_f[0:16, 0:16]),
                 reads=[Ttb], writes=[psb[2]])
            S.op("dve", lambda e: e.tensor_copy(out=TT[:, :], in_=ps[2][0:32, 0:16]), reads=[psb[2]], writes=[TTb])
            S.op("pe", lambda e: e.matmul(ps[3][0:16, 0:32], lhsT=TT[:, :], rhs=mmat[:, :], start=True, stop=True),
                 reads=[TTb], writes=[psb[3]])
            S.op("dve", lambda e: e.tensor_copy(out=Dsb[:, :], in_=ps[3][0:16, 0:32]), reads=[psb[3]], writes=[Dsbb])
            S.op("dve", lambda e: e.tensor_tensor(
                out=Sc[:, NM:L].rearrange("p (s t) -> p s t", t=128), in0=Sc[:, NM:L].rearrange("p (s t) -> p s t", t=128),
                in1=Dsb[:, :].unsqueeze(2).to_broadcast([16, 32, 128]), op=ALU.add),
                reads=[Scb, Dsbb], writes=[Scb])
            S.op("pe", lambda e: e.transpose(out=ps[4][0:16, 0:16], in_=Sc[:, 0:NM], identity=ident_f[0:16, 0:16]),
                 reads=[Scb], writes=[psb[4]])
            S.op("dve", lambda e: e.tensor_copy(out=negc_tok[0:16, 0, :], in_=ps[4][0:16, 0:16]),
                 reads=[psb[4]], writes_nc=[nctb])
            for s in range(32):
                S.op("pe", lambda e, s=s: e.transpose(out=ps[5][:, 16 * s:16 * s + 16],
                                                      in_=Sc[:, NM + 128 * s:NM + 128 * (s + 1)],
                                                      identity=ident_f[0:16, 0:16]),
                     reads=[Scb], writes=[psb[5]], sig=(s == 31))
            S.op("dve", lambda e: e.tensor_copy(out=negc_tok[:, 1:33, :].rearrange("p s h -> p (s h)"), in_=ps[5][:, :]),
                 reads=[psb[5]], writes_nc=[nctb])
            S.op("dve", lambda e: e.tensor_scalar(out=sig8[:, :], in0=Sc[:, NM:NM + NOWN], scalar1=-8.0, scalar2=None,
                                                  op0=ALU.mult), reads=[Scb], writes=[sig8b])
            for h_ in range(16):
                S.dma("sp", "sig8r", sig8r[8 * h_:8 * h_ + 8, :], sig8[h_:h_ + 1, :].rearrange("o (a b) -> o a b", b=256),
                      reads=[sig8b], writes_nc=[sig8rb])
            for f_ in proj_chunks(0):
                f_()
            g_proj(0)
            if NPAIR > 1:
                load_kqv(1)
            S.barrier()
            S.cps.pop()
            dump("negc_tok", negc_tok[:].rearrange("p s h -> p (s h)"), [128, 33 * 16], F32)
            dump("sig8", sig8[:], [16, NOWN], BF16)
            dump("xnT0", xnT[:, 0, :], [128, L], BF16)
            dump("xnT7", xnT[:, 7, :], [128, L], BF16)
            dump("xnh3", xnh[:, 3, :], [128, NHALO], BF16)
            dump("Sc", Sc[:], [16, L], F32)
            S.barrier()


        KA[1] = sb(pa, "KA1", [65, L], BF16)
        KB[1] = sb(pa, "KB1", [65, L], BF16)
        QA[1] = sb(pa, "QA1", [65, NOWN], BF16)
        QB[1] = sb(pa, "QB1", [65, NOWN], BF16)
        V1[1] = sb(pa, "V1_1", [128, 33, 192], BF16)
        sg[1] = sb(pa, "sg1", [128, NOWN], BF16)
        memset_aug(1)
        for hp in range(NPAIR):
            if hp > 0:
                g_proj(hp)
            if hp + 1 < NPAIR:
                pre = [lambda hp=hp: load_g(hp + 1)]
                if 1 <= hp <= 4:
                    def wobf_cast(hp=hp):
                        for q8 in (2 * (hp - 1), 2 * (hp - 1) + 1):
                            S.dma("pool", "wobf", wo_bf[256 * q8:256 * (q8 + 1), :], w_out[256 * q8:256 * (q8 + 1), :])
                    pre.append(wobf_cast)
                attention(hp, pre, proj_chunks(hp + 1), [lambda hp=hp: load_kqv(hp + 2)] if hp + 2 < NPAIR else [])
            else:
                def pool_w_prefetch():
                    S.dma("pool", "wu0", wq[:, 0, :, :],
                          w_in[:, C_U + 128 * 6:C_U + 128 * 7].rearrange("(c p) n -> p c n", p=128), writes_nc=[wkqvb, wgb])
                    S.dma("pool", "wu0", wq[:, 1, :, :],
                          w_in[:, C_GP + 128 * 6:C_GP + 128 * 7].rearrange("(c p) n -> p c n", p=128), writes_nc=[wkqvb, wgb])
                    S.dma("pool", "wpl", wq[:, 2:4, :, :].rearrange("p a c n -> p (a c n)").rearrange("p (g c e) -> p g c e", g=4, c=2),
                          w_pool.rearrange("g (c p) e -> p g c e", p=128), writes_nc=[wkqvb, wgb])
                attention(hp, [pool_w_prefetch], [])
        S.barrier()
        S.cps.pop()
        dump("KA", KA[(NPAIR - 1) % 2][:], [65, L], BF16)
        dump("KB", KB[(NPAIR - 1) % 2][:], [65, L], BF16)
        dump("QA", QA[(NPAIR - 1) % 2][:], [65, NOWN], BF16)
        dump("QB", QB[(NPAIR - 1) % 2][:], [65, NOWN], BF16)
        dump("V1", V1[(NPAIR - 1) % 2][:].rearrange("p s c -> p (s c)"), [128, 33 * 192], BF16)
        dump("sg", sg[(NPAIR - 1) % 2][:], [128, NOWN], BF16)
        dump("mixA", mixA[:].rearrange("p a t -> p (a t)"), [128, 8 * NOWN], BF16)
        S.barrier()

    es_p = ExitStack()
    mixP = sb(es_p, "mixP", [128, 8, NOWN], BF16)
    with ExitStack() as pp:
        wu = [wq[:, 0:2, :, :], sb(pp, "wu1", [128, 2, 8, 128], BF16)]
        wub = S.bufs_n(2)
        wpl = wq[:, 2:4, :, :].rearrange("p a c n -> p (a c n)").rearrange("p (g c e) -> p g c e", g=4, c=2)
        wplb = S.buf()
        U = [sb(pp, f"U{i}", [128, 8, 144], F32) for i in range(2)]
        Ub = S.bufs_n(2)
        T1 = sb(pp, "T1", [128, 8, 144], F32)
        T2 = sb(pp, "T2", [128, 8, 144], F32)
        T1b, T2b = S.bufs_n(2)
        dT = sb(pp, "dT", [128, 2, NOWN], BF16)
        dTb = S.bufs_n(2)
        sgp = [sb(pp, f"sgp{i}", [128, NOWN], BF16) for i in range(2)]
        sgpb = S.bufs_n(2)
        pTb = S.buf()

        def load_wu(ct):
            k = ct % 2
            S.dma("pool", f"wu{k}", wu[k][:, 0, :, :],
                  w_in[:, C_U + 128 * ct:C_U + 128 * (ct + 1)].rearrange("(c p) n -> p c n", p=128), writes_nc=[wub[k]])
            S.dma("pool", f"wu{k}", wu[k][:, 1, :, :],
                  w_in[:, C_GP + 128 * ct:C_GP + 128 * (ct + 1)].rearrange("(c p) n -> p c n", p=128), writes_nc=[wub[k]])

        pc = [0]

        def pbank():
            b = pc[0] % 4
            pc[0] += 1
            return b

        hcount = 0
        pending = []
        xnTb = S.buf()
        es_w = ExitStack()
        wo_box = []
        CT_ORDER = [6, 7, 4, 5, 2, 3, 0, 1]
        for cti, ct in enumerate(CT_ORDER):
            k = ct % 2
            gi = ct // 2
            nlev = gi + 1
            w = 2 ** nlev
            if cti + 1 < 8:
                load_wu(CT_ORDER[cti + 1])
            W = wu[k]
            for hf in range(2):
                uk = hcount % 2
                hcount += 1
                Uh, Uhb = U[uk], Ub[uk]
                for t in (2 * hf, 2 * hf + 1):
                    c0 = NM + 512 * t
                    bank = pbank()
                    for c in range(8):
                        S.op("pe", lambda e, bank=bank, c=c, c0=c0, W=W: e.matmul(
                            ps[bank][:, :], lhsT=W[:, 0, c, :], rhs=xnT[:, c, c0:c0 + 512], start=(c == 0), stop=(c == 7)),
                            reads=[wub[k], xnTb], writes=[psb[bank]], sig=(c == 7))
                    b0 = 4 * (t - 2 * hf)
                    S.op("dve", lambda e, bank=bank, b0=b0, Uh=Uh: e.tensor_copy(
                        out=Uh[:, b0:b0 + 4, 16:144], in_=ps[bank][:, :].rearrange("p (s c) -> p s c", c=128)),
                        reads=[psb[bank]], writes_nc=[Uhb])
                bank = pbank()
                for c in range(8):
                    S.op("pe", lambda e, bank=bank, c=c, W=W, hf=hf: e.matmul(
                        ps[bank][:, 0:128], lhsT=W[:, 0, c, :], rhs=xnh[:, c, 128 * hf:128 * (hf + 1)], start=(c == 0), stop=(c == 7)),
                        reads=[wub[k], xnTb], writes=[psb[bank]], sig=(c == 7))
                S.op("dve", lambda e, bank=bank, Uh=Uh: e.tensor_copy(
                    out=Uh[:, :, 0:16], in_=ps[bank][:, 0:128].rearrange("p (s c) -> p s c", c=16)),
                    reads=[psb[bank]], writes_nc=[Uhb])
                if hf == 0 and pending:
                    pending.pop(0)()
                for t in (2 * hf, 2 * hf + 1):
                    c0 = NM + 512 * t
                    bank = pbank()
                    for c in range(8):
                        S.op("pe", lambda e, bank=bank, c=c, c0=c0, W=W: e.matmul(
                            ps[bank][:, :], lhsT=W[:, 1, c, :], rhs=xnT[:, c, c0:c0 + 512], start=(c == 0), stop=(c == 7)),
                            reads=[wub[k], xnTb], writes=[psb[bank]], sig=(c == 7))
                    S.op("act", lambda e, bank=bank, t=t, k=k: e.activation(out=sgp[k][:, 512 * t:512 * (t + 1)], in_=ps[bank][:, :], func=AF.Silu),
                         reads=[psb[bank]], writes_nc=[sgpb[k]])
                src, srcb = Uh, Uhb
                tmp = [(T1, T1b), (T2, T2b)]
                for lv in range(nlev):
                    sh = 2 ** lv
                    lo = 16 - (w - 2 * sh)
                    dst, dstb = tmp[lv % 2]
                    weng = "pool" if (lv + hcount) % 2 == 0 else "dve"
                    S.op(weng, lambda e, src=src, dst=dst, sh=sh, lo=lo: e.tensor_tensor(
                        out=dst[:, :, lo:144], in0=src[:, :, lo:144], in1=src[:, :, lo - sh:144 - sh], op=ALU.add),
                        reads=[srcb], writes_nc=[dstb])
                    src, srcb = dst, dstb
                S.op("dve", lambda e, src=src, k=k, w=w, hf=hf, Uh=Uh: e.scalar_tensor_tensor(
                    out=dT[:, k, 1024 * hf:1024 * (hf + 1)].rearrange("p (s c) -> p s c", c=128), in0=src[:, :, 16:144], scalar=1.0 / w,
                    in1=Uh[:, :, 16:144], op0=ALU.mult, op1=ALU.subtract),
                    reads=[srcb, Uhb], writes_nc=[dTb[k]])
                if cti == 7 and hf == 1:
                    es_x.close()
                    wo_ = sb(es_w, "wo", [128, 16, D], BF16, side="right")
                    wobs_ = S.bufs_n(4)
                    wo_box.append((wo_, wobs_))
                    for q4 in range(4):
                        S.dma("sp", f"wo{q4}", wo_[:, 4 * q4:4 * q4 + 4, :],
                              wo_bf[512 * q4:512 * (q4 + 1), :].rearrange("(c p) n -> p c n", p=128),
                              writes_nc=[xnTb, wobs_[q4]])
            if k == 1:
                def wpool_emit(gi=gi):
                    for et_ in range(2):
                        ctp = 2 * gi + et_
                        for t in range(4):
                            bank = 4 + (pc[0] % 4)
                            pc[0] += 1
                            for cc in range(2):
                                S.op("pe", lambda e, bank=bank, cc=cc, gi=gi, et_=et_, t=t: e.matmul(
                                    ps[bank][:, :], lhsT=wpl[:, gi, cc, 128 * et_:128 * (et_ + 1)],
                                    rhs=dT[:, cc, 512 * t:512 * (t + 1)], start=(cc == 0), stop=(cc == 1)),
                                    reads=[wplb, dTb[0], dTb[1]], writes=[psb[bank]], sig=(cc == 1))
                            S.op("dve", lambda e, bank=bank, ctp=ctp, et_=et_, t=t: e.scalar_tensor_tensor(
                                out=mixP[:, ctp, 512 * t:512 * (t + 1)], in0=ps[bank][:, :], scalar=cv[:, 16 + ctp:17 + ctp],
                                in1=sgp[et_][:, 512 * t:512 * (t + 1)], op0=ALU.mult, op1=ALU.mult),
                                reads=[psb[bank], sgpb[et_]], writes_nc=[pTb])
                pending.append(wpool_emit)
        while pending:
            pending.pop(0)()
        S.barrier(skip_slots=("wo0", "wo1", "wo2", "wo3"), keep_bufs=wo_box[0][1])
    wo, wobs = wo_box[0]

    with ExitStack() as po:
        lnb = sb(po, "lnb", [128, 4, D], F32)
        lnbb = S.buf()
        NX = 3
        o_xt = [sb(po, f"oxt{i}", [128, D], F32) for i in range(NX)]
        o_xtb = S.bufs_n(NX)
        x0 = [sb(po, f"x0{i}", [128, D], F32) for i in range(NX)]
        x0b = S.bufs_n(NX)
        zt = [sb(po, f"zt{i}", [128, D], F32) for i in range(2)]
        ztb = S.bufs_n(2)
        ot = [sb(po, f"ot{i}", [128, D], F32) for i in range(2)]
        otb = S.bufs_n(2)
        NS = 6
        o_st = [sb(po, f"ost{i}", [128, 2, 6], F32) for i in range(NS)]
        o_mv = [sb(po, f"omv{i}", [128, 2], F32) for i in range(NS)]
        o_sd = [sb(po, f"osd{i}", [128, 1], F32) for i in range(NS)]
        o_rs = [sb(po, f"ors{i}", [128, 1], F32) for i in range(NS)]
        o_stb, o_mvb, o_sdb, o_rsb = S.bufs_n(NS), S.bufs_n(NS), S.bufs_n(NS), S.bufs_n(NS)
        for i4 in range(4):
            S.dma("sp", "lnb", lnb[:, i4, :], lnrows_d[i4:i4 + 1, :].partition_broadcast(128)[:, 0, :], writes_nc=[lnbb])
        S.op("pool", lambda e: e.tensor_scalar(out=lnb[:, 0:2, :], in0=lnb[:, 0:2, :], scalar1=ALPHA, scalar2=None, op0=ALU.mult),
             reads=[lnbb], writes=[lnbb])

        def ln_stats(kk, src):
            S.op("dve", lambda e: e.bn_stats(out=o_st[kk][:, 0, :], in_=src[0][:, 0:512]), reads=[src[1]], writes=[o_stb[kk]])
            S.op("dve", lambda e: e.bn_stats(out=o_st[kk][:, 1, :], in_=src[0][:, 512:1024]), reads=[src[1]], writes_nc=[o_stb[kk]])
            S.op("dve", lambda e: e.bn_aggr(out=o_mv[kk][:, :], in_=o_st[kk][:].rearrange("p a b -> p (a b)")),
                 reads=[o_stb[kk]], writes=[o_mvb[kk]])
            S.op("act", lambda e: e.activation(out=o_sd[kk][:, :], in_=o_mv[kk][:, 1:2], func=AF.Sqrt, bias=epsT[:, 0:1], scale=1.0),
                 reads=[o_mvb[kk]], writes=[o_sdb[kk]])

        def ln_stats_b(kk):
            S.op("dve", lambda e: e.reciprocal(out=o_rs[kk][:, :], in_=o_sd[kk][:, :]), reads=[o_sdb[kk]], writes=[o_rsb[kk]])

        def prep(i):
            k = i % NX
            kk = i % 3
            S.dma("sp", f"oxt{k}", o_xt[k][:, :], xs[NM + 128 * i:NM + 128 * (i + 1), :], writes=[o_xtb[k]])
            S.op("dve", lambda e, k=k, i=i: e.tensor_scalar(
                out=x0[k][:, :], in0=o_xt[k][:, :], scalar1=mv_all[:, 1 + i, 0:1], scalar2=rs_all[:, 1 + i:2 + i],
                op0=ALU.subtract, op1=ALU.mult), reads=[o_xtb[k]], writes=[x0b[k]])
            S.op("pool", lambda e, k=k: e.tensor_tensor(out=x0[k][:, :], in0=x0[k][:, :], in1=lnb[:, 0, :], op=ALU.mult),
                 reads=[x0b[k], lnbb], writes=[x0b[k]])
            S.op("pool", lambda e, k=k: e.tensor_tensor(out=x0[k][:, :], in0=x0[k][:, :], in1=lnb[:, 1, :], op=ALU.add),
                 reads=[x0b[k], lnbb], writes=[x0b[k]])

        prep(0)
        prep(1)
        for i in range(16):
            k = i % NX
            k2 = i % 2
            for half in range(2):
                bank = (2 * i + half) % 8
                for c in range(16):
                    S.op("pe", lambda e, bank=bank, c=c, i=i, half=half: e.matmul(
                        ps[bank][:, :], lhsT=(mixA if c < 8 else mixP)[:, c % 8, 128 * i:128 * (i + 1)],
                        rhs=wo[:, c, 512 * half:512 * (half + 1)],
                        start=(c == 0), stop=(c == 15)), reads=[wobs[c // 4]], writes=[psb[bank]], sig=(c == 15))
                S.op("dve", lambda e, bank=bank, k=k, k2=k2, half=half: e.tensor_tensor(
                    out=zt[k2][:, 512 * half:512 * (half + 1)], in0=ps[bank][:, :], in1=x0[k][:, 512 * half:512 * (half + 1)],
                    op=ALU.add), reads=[psb[bank], x0b[k]], writes_nc=[ztb[k2]])
            kk = 3 + (i % 3)
            ln_stats(kk, (zt[k2], ztb[k2]))
            if i + 2 < 16:
                prep(i + 2)
            ln_stats_b(kk)
            S.op("dve", lambda e, k2=k2, kk=kk: e.scalar_tensor_tensor(
                out=ot[k2][:, :], in0=zt[k2][:, :], scalar=o_mv[kk][:, 0:1], in1=lnb[:, 2, :],
                op0=ALU.subtract, op1=ALU.mult), reads=[ztb[k2], o_mvb[kk], lnbb], writes=[otb[k2]])
            S.op("dve", lambda e, k2=k2, kk=kk: e.scalar_tensor_tensor(
                out=ot[k2][:, :], in0=ot[k2][:, :], scalar=o_rs[kk][:, 0:1], in1=lnb[:, 3, :],
                op0=ALU.mult, op1=ALU.add), reads=[otb[k2], o_rsb[kk], lnbb], writes=[otb[k2]])
            S.dma("act", f"oy{k2}", y[128 * i:128 * (i + 1), :], ot[k2][:, :], reads=[otb[k2]])
        S.barrier()

    with ExitStack() as semstack:
        sems = {}
        for e in Sched.ENG:
            sems[("E", e)] = semstack.enter_context(nc.semaphore(f"sem_{e}"))
        for slot in S.dcount:
            sems[("D", slot)] = semstack.enter_context(nc.semaphore(f"semd_{slot}"))
        with nc.Block() as block:
            S.replay(nc, block, sems, upto)
    es_w.close()
    es_p.close()
    es.close()
    return nc


def _core_inputs(c, x, meta_tokens, shared):
    b, r = c // 2, c % 2
    hfull = np.concatenate([meta_tokens, x[b]], axis=0)
    own_g = [2 * i + r for i in range(16)]
    oth_g = [2 * j + 1 - r for j in range(16)]
    xb = x[b].reshape(32, 128, D)
    halo = np.stack([hfull[128 * g:128 * g + 16] for g in own_g], axis=0).reshape(NHALO, D)
    xs = np.concatenate([meta_tokens, xb[own_g].reshape(-1, D), xb[oth_g].reshape(-1, D), halo], axis=0)
    gl = np.array(own_g + oth_g)
    sl = np.arange(32)
    mm = (gl[:, None] < gl[None, :]).astype(np.float32) - (sl[:, None] < sl[None, :]).astype(np.float32)
    kq = np.arange(128)
    tri = np.where(kq[:, None] <= kq[None, :], 0.0, MASKV).astype(np.float32)
    mx = np.full((128, 128), MASKV if r == 0 else 0.0, np.float32)
    d = dict(shared)
    d["xs"] = np.ascontiguousarray(xs, dtype=np.float32)
    d["mmat"] = np.ascontiguousarray(mm)
    d["masks"] = np.ascontiguousarray(np.concatenate([tri, mx], axis=1))
    return d, own_g


def kernel(x, meta_tokens, ln_in_g, ln_in_b, w_in, b_forget, w_pool, pool_scale, w_out, ln_g, ln_b):
    x = np.asarray(x, np.float32)
    meta_tokens = np.asarray(meta_tokens, np.float32)
    f = lambda a: np.ascontiguousarray(np.asarray(a, np.float32))
    cvec = np.concatenate([f(ln_in_g).reshape(8, 128).T, f(ln_in_b).reshape(8, 128).T,
                           f(pool_scale)[0].reshape(8, 128).T], axis=1)
    shared = {
        "w_in": f(w_in)[0], "w_pool": f(w_pool)[0], "w_out": f(w_out)[0],
        "cvec": np.ascontiguousarray(cvec), "bfv": f(b_forget)[0].reshape(16, 1),
        "ident": np.eye(128, dtype=np.float32),
        "lnrows": np.ascontiguousarray(np.stack([f(ln_in_g), f(ln_in_b), f(ln_g)[0], f(ln_b)[0]], axis=0)),
    }
    in_maps, owns = [], []
    for c in range(8):
        d, own_g = _core_inputs(c, x, meta_tokens, shared)
        in_maps.append(d)
        owns.append(own_g)
    nc = build_nc()
    res = run_bass_kernel_spmd(nc, in_maps, core_ids=list(range(8)))
    out = np.empty((4, SEQ, D), np.float32)
    for c in range(8):
        yb = np.asarray(res.results[c]["y"], np.float32).reshape(16, 128, D)
        ov = out[c // 2].reshape(32, 128, D)
        ov[owns[c]] = yb
    return out
```

```python
import numpy as np
from contextlib import ExitStack
import concourse.bass as bass
import concourse.mybir as mybir
from concourse.bass_utils import run_bass_kernel_spmd

F32 = mybir.dt.float32
BF16 = mybir.dt.bfloat16
AF = mybir.ActivationFunctionType
ALU = mybir.AluOpType

D = 1024
SEQ = 4096
NM = 16
L = SEQ + NM
H = 16
NOWN = 2048
NHALO = 256
ROWS = NM + SEQ + NHALO
INCOLS = 6160
C_Q, C_K, C_V, C_F, C_G, C_U, C_GP = 0, 1024, 2048, 3072, 3088, 4112, 5136
ALPHA = 2.0 ** 0.25
EPS = 1e-5
MASKV = -30000.0
LA = 2


class Buf:
    __slots__ = ("w", "r")

    def __init__(self):
        self.w = {}
        self.r = {}


class Sched:
    ENG = ("pe", "act", "dve", "pool", "sp")

    def __init__(self):
        self.q = {n: [] for n in self.ENG}
        self.dcount = {}
        self.bufs = []
        self.cps = []

    def buf(self):
        b = Buf()
        self.bufs.append(b)
        return b

    def bufs_n(self, n):
        return [self.buf() for _ in range(n)]

    def _deps(self, eng, reads, writes, writes_nc):
        waits = {}

        def add(k, v, war=False):
            if k[0] == "E" and k[1] == eng and eng == "pe":
                return
            if waits.get(k, -1) < v:
                waits[k] = v

        for b in reads:
            for k, v in b.w.items():
                add(k, v)
        for b in writes:
            for k, v in b.w.items():
                add(k, v)
            for k, v in b.r.items():
                add(k, v, war=True)
        for b in writes_nc:
            for k, v in b.r.items():
                add(k, v, war=True)
        return waits

    @staticmethod
    def _mark(tok, reads, writes, writes_nc):
        k = (tok[0], tok[1])
        for b in reads:
            if b.r.get(k, -1) < tok[2]:
                b.r[k] = tok[2]
        for b in writes:
            b.w = {k: tok[2]}
            b.r = {}
        for b in writes_nc:
            if b.w.get(k, -1) < tok[2]:
                b.w[k] = tok[2]

    def op(self, eng, fn, reads=(), writes=(), writes_nc=(), sig=True):
        waits = self._deps(eng, reads, writes, writes_nc)
        tok = ("E", eng, len(self.q[eng]))
        self.q[eng].append(dict(kind="op", fn=fn, waits=waits, sig=sig))
        self._mark(tok, reads, writes, writes_nc)
        return tok

    def dma(self, eng, slot, out, in_, reads=(), writes=(), writes_nc=(), **kw):
        waits = self._deps(eng, reads, writes, writes_nc)
        self.dcount[slot] = self.dcount.get(slot, 0) + 16
        tok = ("D", slot, self.dcount[slot])
        self.q[eng].append(dict(kind="dma", out=out, in_=in_, waits=waits, slot=slot, sig=False, kw=kw))
        self._mark(tok, reads, writes, writes_nc)
        return tok

    def barrier(self, skip_slots=(), keep_bufs=()):
        toks = []
        for e in self.ENG:
            ops = self.q[e]
            last = None
            for i in range(len(ops) - 1, -1, -1):
                if ops[i]["kind"] == "op":
                    last = i
                    break
            if last is not None:
                ops[last]["sig"] = True
                toks.append(("E", e, last))
        for slot, v in self.dcount.items():
            if slot not in skip_slots:
                toks.append(("D", slot, v))
        for e in self.ENG:
            waits = {}
            for t in toks:
                if t[0] == "E" and t[1] == e:
                    continue
                waits[(t[0], t[1])] = t[2]
            self.q[e].append(dict(kind="wait", waits=waits, sig=False))
        for b in self.bufs:
            if any(b is kb for kb in keep_bufs):
                continue
            b.w = {}
            b.r = {}
        self.cps.append({e: len(self.q[e]) for e in self.ENG})

    def replay(self, nc, block, sems, upto=None):
        if upto is not None:
            for e in self.ENG:
                self.q[e] = self.q[e][:self.cps[upto][e]]
        vals = {}
        for e, ops in self.q.items():
            cnt = 0
            for o in ops:
                if o["kind"] == "op" and o["sig"]:
                    cnt += 1
                o["cum"] = cnt
            nxt = None
            v = [None] * len(ops)
            for i in range(len(ops) - 1, -1, -1):
                if ops[i]["kind"] == "op" and ops[i]["sig"]:
                    nxt = ops[i]["cum"]
                v[i] = nxt
            vals[e] = v

        def run(e, handle):
            waited = {}
            own = sems[("E", e)]
            for o in self.q[e]:
                for key, val in o["waits"].items():
                    if key[0] == "E":
                        val = vals[key[1]][val]
                        assert val is not None
                    if waited.get(key, -1) >= val:
                        continue
                    waited[key] = val
                    handle.wait_ge(sems[key], val)
                if o["kind"] == "op":
                    ins = o["fn"](handle)
                    if o["sig"]:
                        ins.then_inc(own, 1)
                elif o["kind"] == "dma":
                    handle.dma_start(out=o["out"], in_=o["in_"], **o["kw"]).then_inc(sems[("D", o["slot"])], 16)

        block.tensor(lambda h: run("pe", h))
        block.scalar(lambda h: run("act", h))
        block.vector(lambda h: run("dve", h))
        block.gpsimd(lambda h: run("pool", h))
        block.sync(lambda h: run("sp", h))


def build_nc(upto=None):
    nc = bass.Bass("TRN2", target_bir_lowering=False)
    S = Sched()

    def din(name, shape):
        return nc.dram_tensor(name, shape, F32, kind="ExternalInput").ap()

    xs = din("xs", [ROWS, D])
    w_in = din("w_in", [D, INCOLS])
    w_pool = din("w_pool", [4, 256, 256])
    w_out = din("w_out", [2048, D])
    cvec_d = din("cvec", [128, 24])
    bfv_d = din("bfv", [16, 1])
    mmat_d = din("mmat", [32, 32])
    ident_d = din("ident", [128, 128])
    masks_d = din("masks", [128, 256])
    lnrows_d = din("lnrows", [4, D])
    y = nc.dram_tensor("y", [NOWN, D], F32, kind="ExternalOutput").ap()
    wo_bf = nc.dram_tensor("wo_bf", [2048, D], BF16, kind="Internal").ap()

    DUMP = False

    def dump(name, ap, shape, dt):
        if not DUMP:
            return
        o = nc.dram_tensor("dbg_" + name, list(shape), dt, kind="ExternalOutput").ap()
        S.dma("sp", "dbg", o, ap)
        S.barrier()
        S.cps.pop()

    es = ExitStack()

    def sb(stack, name, shape, dt, side=None):
        if side is None:
            return stack.enter_context(nc.sbuf_tensor(name, shape, dt))
        return stack.enter_context(nc.sbuf_tensor(name, shape, dt, side=side))

    ps = [es.enter_context(nc.psum_tensor(f"ps{i}", [128, 512], F32)) for i in range(8)]
    psb = S.bufs_n(8)

    mixA = sb(es, "mixA", [128, 8, NOWN], BF16)
    ident_f = sb(es, "ident_f", [128, 128], F32)
    ident_b = sb(es, "ident_b", [128, 128], BF16)
    masks_b = sb(es, "masks_b", [128, 256], BF16)
    cv = sb(es, "cv", [128, 24], F32)
    epsT = sb(es, "epsT", [128, 1], F32)
    oneT = sb(es, "oneT", [128, 1], F32)
    nbf = sb(es, "nbf", [16, 1], F32)
    mmat = sb(es, "mmat_s", [32, 32], F32)
    negc_tok = sb(es, "negc_tok", [128, 33, 16], F32)
    sig8r = sb(es, "sig8r", [128, 256], BF16)
    constb = S.buf()
    wq = sb(es, "wqkvg", [128, 4, 8, 128], BF16)
    NT0 = 35
    mv_all = sb(es, "mv_all", [128, NT0, 2], F32)
    sd_all = sb(es, "sd_all", [128, NT0], F32)
    rs_all = sb(es, "rs_all", [128, NT0], F32)
    s1_all = sb(es, "s1_all", [128, NT0], F32)
    s2_all = sb(es, "s2_all", [128, NT0], F32)
    t1_all = sb(es, "t1_all", [128, NT0], F32)

    S.dma("sp", "const", ident_f[:], ident_d[:, :], writes=[])
    S.dma("sp", "const", cv[:], cvec_d[:, :], writes=[])
    S.dma("sp", "const", nbf[:], bfv_d[:, :], writes=[])
    S.dma("sp", "const", mmat[:], mmat_d[:, :], writes=[])
    S.dma("pool", "constp", ident_b[:], ident_d[:, :], writes=[])
    S.dma("pool", "constp", masks_b[:], masks_d[:, :], writes=[])
    S.op("dve", lambda e: e.memset(negc_tok[:, 0, :], MASKV))
    S.op("dve", lambda e: e.memset(epsT[:], EPS))
    S.op("dve", lambda e: e.memset(oneT[:], 1.0))
    S.barrier()
    S.op("dve", lambda e: e.tensor_scalar(out=nbf[:], in0=nbf[:], scalar1=-1.0, scalar2=None, op0=ALU.mult))

    es_x = ExitStack()
    xnT = sb(es_x, "xnT", [128, 8, L], BF16, side="right")
    xnh = sb(es_x, "xnh", [128, 8, NHALO], BF16, side="right")

    for gi_w, cbase_w in enumerate((C_Q, C_K, C_V, C_G)):
        S.dma("pool", "wq", wq[:, gi_w, :, :],
              w_in[:, cbase_w:cbase_w + 128].rearrange("(c p) n -> p c n", p=128))

    with ExitStack() as p0:
        NXT = 6
        NXH = 8
        xt = [sb(p0, f"xt{i}", [128, D], F32) for i in range(NXT)]
        xtb = S.bufs_n(NXT)
        xh = [sb(p0, f"xh{i}", [128, D], BF16) for i in range(NXH)]
        psbf = [ps[i][:, :].bitcast(BF16) for i in range(4)]
        xhb = S.bufs_n(NXH)
        st = [sb(p0, f"st{i}", [128, 2, 6], F32) for i in range(NXT)]
        stb = S.bufs_n(NXT)
        mvb, sdb, rsb = S.bufs_n(NT0), S.bufs_n(NT0), S.bufs_n(NT0)

        groups = [[(0, NM, xnT, 0)]]
        for g4 in range(8):
            groups.append([(NM + 128 * s, 128, xnT, NM + 128 * s) for s in range(4 * g4, 4 * g4 + 4)])
        groups.append([(NM + SEQ, 128, xnh, 0), (NM + SEQ + 128, 128, xnh, 128)])
        flat = []
        for gi_, grp in enumerate(groups):
            for j_, (r0, P, dst, c0) in enumerate(grp):
                flat.append((gi_, j_, r0, P))
        bcount = [0]

        junk2 = [sb(p0, f"junk{i}", [128, D], BF16) for i in range(2)]
        junkb2 = S.bufs_n(2)
        s1b, s2b, t1b = S.bufs_n(NT0), S.bufs_n(NT0), S.bufs_n(NT0)

        def stageA1(ti):
            gi_, j_, r0, P = flat[ti]
            k = ti % NXT
            tt_ = ti
            S.dma("sp", f"xt{k}", xt[k][0:P, :], xs[r0:r0 + P, :], writes=[xtb[k]])
            S.op("act", lambda e: e.activation(out=junk2[0][0:P, :], in_=xt[k][0:P, :], func=AF.Identity,
                                               accum_out=s1_all[0:P, tt_:tt_ + 1]),
                 reads=[xtb[k]], writes=[s1b[tt_], junkb2[0]])
            S.op("act", lambda e: e.activation(out=junk2[1][0:P, :], in_=xt[k][0:P, :], func=AF.Square,
                                               accum_out=s2_all[0:P, tt_:tt_ + 1]),
                 reads=[xtb[k]], writes=[s2b[tt_], junkb2[1]])

        def stageA2(ti):
            gi_, j_, r0, P = flat[ti]
            tt_ = ti
            S.op("dve", lambda e: e.tensor_scalar(out=mv_all[0:P, tt_, 0:1], in0=s1_all[0:P, tt_:tt_ + 1], scalar1=1.0 / D,
                                                  scalar2=None, op0=ALU.mult), reads=[s1b[tt_]], writes=[mvb[tt_]])
            S.op("dve", lambda e: e.tensor_tensor(out=t1_all[0:P, tt_:tt_ + 1], in0=mv_all[0:P, tt_, 0:1], in1=mv_all[0:P, tt_, 0:1],
                                                  op=ALU.mult), reads=[mvb[tt_]], writes=[t1b[tt_]])
            S.op("dve", lambda e: e.scalar_tensor_tensor(out=mv_all[0:P, tt_, 1:2], in0=s2_all[0:P, tt_:tt_ + 1], scalar=1.0 / D,
                                                         in1=t1_all[0:P, tt_:tt_ + 1], op0=ALU.mult, op1=ALU.subtract),
                 reads=[s2b[tt_], t1b[tt_]], writes_nc=[mvb[tt_]])
            S.op("act", lambda e: e.activation(out=sd_all[0:P, tt_:tt_ + 1], in_=mv_all[0:P, tt_, 1:2], func=AF.Sqrt,
                                               bias=epsT[0:P, 0:1], scale=1.0),
                 reads=[mvb[tt_]], writes=[sdb[tt_]])

        def stageB(ti):
            gi_, j_, r0, P = flat[ti]
            k = ti % NXT
            kh = ti % NXH
            tt_ = ti
            S.op("dve", lambda e: e.reciprocal(out=rs_all[0:P, tt_:tt_ + 1], in_=sd_all[0:P, tt_:tt_ + 1]),
                 reads=[sdb[tt_]], writes=[rsb[tt_]])
            S.op("dve", lambda e: e.tensor_scalar(
                out=xh[kh][0:P, :], in0=xt[k][0:P, :], scalar1=mv_all[0:P, tt_, 0:1], scalar2=rs_all[0:P, tt_:tt_ + 1],
                op0=ALU.subtract, op1=ALU.mult),
                reads=[xtb[k], mvb[tt_], rsb[tt_]], writes=[xhb[kh]])

        def stageC(gi_, ti_last):
            grp = groups[gi_]
            ti0 = ti_last - len(grp) + 1
            dst, c00 = grp[0][2], grp[0][3]
            ncols = sum(P for (_, P, _, _) in grp)
            for c in range(8):
                bank = bcount[0] % 4
                bcount[0] += 1
                off = 0
                for j, (r0, P, _, _) in enumerate(grp):
                    kh = (ti0 + j) % NXH
                    S.op("pe", lambda e, bank=bank, off=off, c=c, kh=kh, P=P: e.transpose(
                        out=psbf[bank][:, off:off + P], in_=xh[kh][0:P, c * 128:(c + 1) * 128],
                        identity=ident_b[0:P, 0:P]),
                        reads=[xhb[kh]], writes=[psb[bank]], sig=(j == len(grp) - 1))
                    off += P
                S.op("dve", lambda e, bank=bank, c=c, dst=dst, c00=c00, ncols=ncols: e.tensor_scalar(
                    out=dst[:, c, c00:c00 + ncols], in0=psbf[bank][:, 0:ncols],
                    scalar1=cv[:, c:c + 1], scalar2=cv[:, 8 + c:9 + c], op0=ALU.mult, op1=ALU.add),
                    reads=[psb[bank]], writes=[])

        nflat = len(flat)
        stageA1(0)
        stageA1(1)
        stageA1(2)
        stageA2(0)
        for ti in range(nflat):
            if ti + 3 < nflat:
                stageA1(ti + 3)
            if ti + 1 < nflat:
                stageA2(ti + 1)
            stageB(ti)
            gi_, j_, _, _ = flat[ti]
            if j_ == len(groups[gi_]) - 1:
                stageC(gi_, ti)
        S.barrier()

    with ExitStack() as pa:
        wkqvb, wgb = S.bufs_n(2)
        KA = [sb(pa, "KA0", [65, L], BF16), None]
        KB = [sb(pa, "KB0", [65, L], BF16), None]
        QA = [sb(pa, "QA0", [65, NOWN], BF16), None]
        QB = [sb(pa, "QB0", [65, NOWN], BF16), None]
        V1 = [sb(pa, "V1_0", [128, 33, 192], BF16), None]
        sg = [sb(pa, "sg0", [128, NOWN], BF16), None]
        sig8rb = S.buf()
        Kb, Qb, Vb, sgb = S.bufs_n(2), S.bufs_n(2), S.bufs_n(2), S.bufs_n(2)
        aTb = S.buf()
        NP = 3
        Pt = [sb(pa, f"Pt{i}", [128, 512], BF16) for i in range(NP)]
        Ptb = S.bufs_n(NP)
        recb = sb(pa, "recb0", [128, 512], F32)
        recbb = S.buf()
        tt = sb(pa, "tt0", [128, 512], F32)
        ttb = S.buf()

        Vzb = S.bufs_n(2)

        def memset_aug(i2):
            S.op("pool", lambda e: e.memset(KA[i2][64:65, :], 1.0), writes_nc=[Kb[i2]])
            S.op("pool", lambda e: e.memset(KB[i2][64:65, :], 1.0), writes_nc=[Kb[i2]])
            S.op("pool", lambda e: e.memset(V1[i2][:, :, 64:128], 1.0), writes_nc=[Vb[i2]])
            S.op("pool", lambda e: e.memset(V1[i2][:, 0, 0:64], 0.0), writes_nc=[Vb[i2], Vzb[i2]])
            S.op("pool", lambda e: e.memset(V1[i2][:, 0, 128:192], 0.0), writes_nc=[Vb[i2], Vzb[i2]])
        memset_aug(0)

        def load_kqv(hp):
            for gi, cbase in enumerate((C_Q, C_K, C_V)):
                S.dma("pool", "wq", wq[:, gi, :, :],
                      w_in[:, cbase + 128 * hp:cbase + 128 * (hp + 1)].rearrange("(c p) n -> p c n", p=128),
                      writes_nc=[wkqvb])

        def load_g(hp):
            S.dma("pool", "wg", wq[:, 3, :, :],
                  w_in[:, C_G + 128 * hp:C_G + 128 * (hp + 1)].rearrange("(c p) n -> p c n", p=128),
                  writes_nc=[wgb])

        SB = [2, 3, 4]
        OB = [5, 6]
        PB = [0, 1, 7]
        LA_ = 2
        pcount = [0]
        Wpair = wq

        def proj_bank():
            bk = PB[pcount[0] % len(PB)]
            pcount[0] += 1
            return bk

        def mm8(gi, rhs_of, N, state, lo_c, hi_c):
            bank = state["bank"]
            for c in range(lo_c, hi_c):
                S.op("pe", lambda e, bank=bank, c=c: e.matmul(
                    ps[bank][:, 0:N], lhsT=Wpair[:, gi, c, :], rhs=rhs_of(c), start=(c == 0), stop=(c == 7)),
                    reads=[wkqvb if gi < 3 else wgb], writes=[psb[bank]], sig=(c == 7))

        def proj_chunks(hp):
            k = hp % 2
            out = []

            def sig_dma():
                S.dma("pool", f"sigA{k}", QA[k][64:65, :].rearrange("o (a b) -> o a b", b=256), sig8r[16 * hp:16 * hp + 8, :], reads=[sig8rb], writes_nc=[Qb[k]])
                S.dma("pool", f"sigB{k}", QB[k][64:65, :].rearrange("o (a b) -> o a b", b=256), sig8r[16 * hp + 8:16 * hp + 16, :], reads=[sig8rb], writes_nc=[Qb[k]])
            out.append(sig_dma)
            jobs = []
            for t in range((L + 511) // 512):
                c0 = 512 * t
                jobs.append((1, c0, min(512, L - c0), KA[k], KB[k], c0, Kb[k]))
            for t in range(4):
                jobs.append((0, NM + 512 * t, 512, QA[k], QB[k], 512 * t, Qb[k]))
            for (gi, c0, N, TA, TB, d0, tb) in jobs:
                state = {}

                def first(gi=gi, c0=c0, N=N, state=state):
                    state["bank"] = proj_bank()
                    mm8(gi, lambda c: xnT[:, c, c0:c0 + N], N, state, 0, 4)

                def second(gi=gi, c0=c0, N=N, TA=TA, TB=TB, d0=d0, tb=tb, state=state):
                    mm8(gi, lambda c: xnT[:, c, c0:c0 + N], N, state, 4, 8)
                    bank = state["bank"]
                    S.op("dve", lambda e: e.tensor_copy(out=TA[0:64, d0:d0 + N], in_=ps[bank][0:64, 0:N]),
                         reads=[psb[bank]], writes_nc=[tb])
                    S.op("dve", lambda e: e.tensor_copy(out=TB[0:64, d0:d0 + N], in_=ps[bank][64:128, 0:N]),
                         reads=[psb[bank]], writes_nc=[tb])
                out.append(first)
                out.append(second)
            vblocks = [(0, NM, 0)] + [(NM + 128 * s, 128, 1 + s) for s in range(32)]
            for g0 in range(0, 33, 4):
                grp = vblocks[g0:g0 + 4]
                state = {}
                for j, (c0, P, slot) in enumerate(grp):
                    def vblk(j=j, c0=c0, P=P, grp=grp, g0=g0, state=state):
                        if j == 0:
                            state["bank"] = proj_bank()
                        bank = state["bank"]
                        lastj = (j == len(grp) - 1)
                        for c in range(8):
                            S.op("pe", lambda e, c=c: e.matmul(
                                ps[bank][0:P, 128 * j:128 * (j + 1)], lhsT=xnT[:, c, c0:c0 + P], rhs=Wpair[:, 2, c, :],
                                start=(c == 0), stop=(c == 7)),
                                reads=[wkqvb], writes=[psb[bank]], sig=(c == 7 and lastj))
                        if not lastj:
                            return
                        if g0 == 0:
                            S.op("dve", lambda e: e.tensor_copy(out=V1[k][0:NM, 0, 0:64], in_=ps[bank][0:NM, 0:64]),
                                 reads=[psb[bank], Vzb[k]], writes_nc=[Vb[k]])
                            S.op("dve", lambda e: e.tensor_copy(out=V1[k][0:NM, 0, 128:192], in_=ps[bank][0:NM, 64:128]),
                                 reads=[psb[bank], Vzb[k]], writes_nc=[Vb[k]])
                            lo = 1
                        else:
                            lo = 0
                        n = len(grp) - lo
                        if n > 0:
                            s0 = grp[lo][2]
                            for (vo, po_) in ((0, 0), (128, 64)):
                                S.op("dve", lambda e, vo=vo, po_=po_: e.tensor_copy(
                                    out=V1[k][:, s0:s0 + n, vo:vo + 64],
                                    in_=ps[bank][:, 128 * lo:128 * (lo + n)].rearrange("p (s c) -> p s c", c=128)[:, :, po_:po_ + 64]),
                                    reads=[psb[bank]], writes_nc=[Vb[k]])
                    out.append(vblk)
            return out

        def g_proj(hp):
            k = hp % 2
            for t in range(4):
                c0 = NM + 512 * t
                state = {"bank": proj_bank()}
                mm8(3, lambda c, c0=c0: xnT[:, c, c0:c0 + 512], 512, state, 0, 8)
                bank = state["bank"]
                S.op("act", lambda e, bank=bank, t=t, k=k: e.activation(out=sg[k][:, 512 * t:512 * (t + 1)], in_=ps[bank][:, :], func=AF.Silu),
                     reads=[psb[bank]], writes_nc=[sgb[k]])

        def attention(hp, pre, inserts, post=()):
            k = hp % 2
            tl = []
            for hh in range(2):
                for J in range(4):
                    lst = [(128, 0, 0, 0, 512, None)]
                    for i in range(4 * J):
                        lst.append((128, NM + 128 * i, 1 + i, 0, 512, None))
                    for j in range(4 * J):
                        lst.append((128, NM + NOWN + 128 * j, 17 + j, 0, 512, None))
                    for sp_ in range(4):
                        i = 4 * J + sp_
                        lst.append((128, NM + 128 * i, 1 + i, 128 * sp_, 512 - 128 * sp_, 0))
                        lst.append((128, NM + NOWN + 128 * i, 17 + i, 128 * sp_, 512 - 128 * sp_, 1))
                    for n_, it in enumerate(lst):
                        tl.append(((hh, J), it, n_ == 0, n_ == len(lst) - 1))
            nt = len(tl)
            START = 4
            post = list(post)
            ins_i = 0
            obank_of = {}
            for idx in range(nt + LA_):
                if idx < nt:
                    (hh, J), (M, kc0, slot, qoff, N, mask), first, last = tl[idx]
                    Kt = KA[k] if hh == 0 else KB[k]
                    Qt = QA[k] if hh == 0 else QB[k]
                    sbank = SB[idx % len(SB)]
                    q0 = 512 * J + qoff
                    S.op("pe", lambda e, sbank=sbank, M=M, kc0=kc0, q0=q0, N=N, Kt=Kt, Qt=Qt, mask=mask: e.matmul(
                        ps[sbank][0:M, 0:N], lhsT=Kt[0:65, kc0:kc0 + M], rhs=Qt[0:65, q0:q0 + N], start=True,
                        stop=(mask is None)),
                        reads=[Kb[k], Qb[k]], writes=[psb[sbank]], sig=(mask is None))
                    if mask is not None:
                        S.op("pe", lambda e, sbank=sbank, mask=mask: e.matmul(
                            ps[sbank][:, 0:128], lhsT=ident_b[:, :], rhs=masks_b[:, 128 * mask:128 * (mask + 1)],
                            start=False, stop=True),
                            reads=[], writes=[psb[sbank]], sig=True)
                    if idx == 0:
                        for f_ in pre:
                            f_()
                    if idx >= START and ins_i < len(inserts):
                        want = ((idx - START + 1) * len(inserts) + (nt - 4 - START) - 1) // max(1, nt - 4 - START)
                        while ins_i < min(want, len(inserts)):
                            inserts[ins_i]()
                            ins_i += 1
                        if ins_i == len(inserts):
                            while post:
                                post.pop(0)()
                j = idx - LA_
                if j >= 0:
                    (hh, J), (M, kc0, slot, qoff, N, mask), first, last = tl[j]
                    sbank = SB[j % len(SB)]
                    pk = j % NP
                    h = 2 * hp + hh
                    if first:
                        obank_of[(hh, J)] = OB[ogroup[0] % 2]
                        ogroup[0] += 1
                    ob = obank_of[(hh, J)]
                    S.op("act", lambda e, sbank=sbank, pk=pk, M=M, N=N, slot=slot, h=h: e.activation(
                        out=Pt[pk][0:M, 0:N], in_=ps[sbank][0:M, 0:N], func=AF.Exp,
                        bias=negc_tok[0:M, slot, h:h + 1], scale=0.125),
                        reads=[psb[sbank]], writes=[Ptb[pk]])
                    vc0 = 0 if hh == 0 else 64
                    S.op("pe", lambda e, ob=ob, pk=pk, M=M, N=N, slot=slot, vc0=vc0, qoff=qoff, first=first, last=last: e.matmul(
                        ps[ob][:, qoff:qoff + N], lhsT=V1[k][0:M, slot, vc0:vc0 + 128], rhs=Pt[pk][0:M, 0:N],
                        start=first, stop=last),
                        reads=[Ptb[pk], Vb[k]], writes=[psb[ob]], sig=(last or idx >= nt - 1))
                    if last:
                        dr = slice(64, 128) if hh == 0 else slice(0, 64)
                        vr = slice(0, 64) if hh == 0 else slice(64, 128)
                        S.op("dve", lambda e, ob=ob, dr=dr: e.reciprocal(out=recb[dr, :], in_=ps[ob][dr, :]),
                             reads=[psb[ob]], writes=[recbb])
                        S.op("dve", lambda e, ob=ob, dr=dr, vr=vr: e.tensor_tensor(
                            out=tt[vr, :], in0=ps[ob][vr, :], in1=recb[dr, :], op=ALU.mult),
                            reads=[psb[ob], recbb], writes=[ttb])
                        S.op("dve", lambda e, vr=vr, J=J, hp=hp: e.tensor_tensor(
                            out=mixA[vr, hp, 512 * J:512 * (J + 1)], in0=tt[vr, :], in1=sg[k][vr, 512 * J:512 * (J + 1)],
                            op=ALU.mult),
                            reads=[ttb, sgb[k]], writes_nc=[aTb])
            while ins_i < len(inserts):
                inserts[ins_i]()
                ins_i += 1
            while post:
                post.pop(0)()

        ogroup = [0]
        NPAIR = 8
        with ExitStack() as p1:
            wf = sb(p1, "wf", [128, 8, 16], BF16)
            lsp = sb(p1, "lsp", [16, L], F32)
            Sc = sb(p1, "Sc", [16, L], F32)
            et = [sb(p1, f"et{i}", [16, 512], F32) for i in range(2)]
            etb = S.bufs_n(2)
            Tt = sb(p1, "Tt", [16, 32], F32)
            TT = sb(p1, "TT", [32, 16], F32)
            Dsb = sb(p1, "Dsb", [16, 32], F32)
            wfb, lspb, Scb, Ttb, TTb, Dsbb, nctb = S.bufs_n(7)
            sig8 = sb(p1, "sig8", [16, NOWN], BF16)
            sig8b = S.buf()
            S.dma("pool", "wf", wf[:], w_in[:, C_F:C_F + 16].rearrange("(c p) n -> p c n", p=128), writes=[wfb])
            ntile = (L + 511) // 512
            for t in range(ntile):
                c0 = 512 * t
                N = min(512, L - c0)
                bank = 6 + t % 2
                k = t % 2
                for c in range(8):
                    S.op("pe", lambda e, bank=bank, c=c, c0=c0, N=N: e.matmul(
                        ps[bank][0:16, 0:N], lhsT=wf[:, c, :], rhs=xnT[:, c, c0:c0 + N], start=(c == 0), stop=(c == 7)),
                        reads=[wfb], writes=[psb[bank]], sig=(c == 7))
                S.op("act", lambda e, bank=bank, k=k, N=N: e.activation(
                    out=et[k][:, 0:N], in_=ps[bank][0:16, 0:N], func=AF.Exp, bias=nbf[:, 0:1], scale=-1.0),
                    reads=[psb[bank]], writes=[etb[k]])
                S.op("act", lambda e, k=k, c0=c0, N=N: e.activation(
                    out=lsp[:, c0:c0 + N], in_=et[k][:, 0:N], func=AF.Ln, bias=oneT[0:16, 0:1], scale=1.0),
                    reads=[etb[k]], writes_nc=[lspb])
            S.op("dve", lambda e: e.tensor_tensor_scan(
                out=Sc[:, :], data0=oneT[0:16, 0:1].to_broadcast([16, L]), data1=lsp[:, :], initial=0.0,
                op0=ALU.mult, op1=ALU.add), reads=[lspb], writes=[Scb])
            S.op("dve", lambda e: e.tensor_tensor(
                out=Tt[:, :], in0=Sc[:, 143:143 + 128 * 31 + 1:128], in1=Sc[:, 15:15 + 128 * 31 + 1:128], op=ALU.subtract),
                reads=[Scb], writes=[Ttb])
            S.op("pe", lambda e: e.transpose(out=ps[2][0:32, 0:16], in_=Tt[:, :], identity=ident_f[0:16, 0:16]),
                 reads=[Ttb], writes=[psb[2]])
            S.op("dve", lambda e: e.tensor_copy(out=TT[:, :], in_=ps[2][0:32, 0:16]), reads=[psb[2]], writes=[TTb])
            S.op("pe", lambda e: e.matmul(ps[3][0:16, 0:32], lhsT=TT[:, :], rhs=mmat[:, :], start=True, stop=True),
                 reads=[TTb], writes=[psb[3]])
            S.op("dve", lambda e: e.tensor_copy(out=Dsb[:, :], in_=ps[3][0:16, 0:32]), reads=[psb[3]], writes=[Dsbb])
            S.op("dve", lambda e: e.tensor_tensor(
                out=Sc[:, NM:L].rearrange("p (s t) -> p s t", t=128), in0=Sc[:, NM:L].rearrange("p (s t) -> p s t", t=128),
                in1=Dsb[:, :].unsqueeze(2).to_broadcast([16, 32, 128]), op=ALU.add),
                reads=[Scb, Dsbb], writes=[Scb])
            S.op("pe", lambda e: e.transpose(out=ps[4][0:16, 0:16], in_=Sc[:, 0:NM], identity=ident_f[0:16, 0:16]),
                 reads=[Scb], writes=[psb[4]])
            S.op("dve", lambda e: e.tensor_copy(out=negc_tok[0:16, 0, :], in_=ps[4][0:16, 0:16]),
                 reads=[psb[4]], writes_nc=[nctb])
            for s in range(32):
                S.op("pe", lambda e, s=s: e.transpose(out=ps[5][:, 16 * s:16 * s + 16],
                                                      in_=Sc[:, NM + 128 * s:NM + 128 * (s + 1)],
                                                      identity=ident_f[0:16, 0:16]),
                     reads=[Scb], writes=[psb[5]], sig=(s == 31))
            S.op("dve", lambda e: e.tensor_copy(out=negc_tok[:, 1:33, :].rearrange("p s h -> p (s h)"), in_=ps[5][:, :]),
                 reads=[psb[5]], writes_nc=[nctb])
            S.op("dve", lambda e: e.tensor_scalar(out=sig8[:, :], in0=Sc[:, NM:NM + NOWN], scalar1=-8.0, scalar2=None,
                                                  op0=ALU.mult), reads=[Scb], writes=[sig8b])
            for h_ in range(16):
                S.dma("sp", "sig8r", sig8r[8 * h_:8 * h_ + 8, :], sig8[h_:h_ + 1, :].rearrange("o (a b) -> o a b", b=256),
                      reads=[sig8b], writes_nc=[sig8rb])
            for f_ in proj_chunks(0):
                f_()
            g_proj(0)
            if NPAIR > 1:
                load_kqv(1)
            S.barrier()
            S.cps.pop()
            dump("negc_tok", negc_tok[:].rearrange("p s h -> p (s h)"), [128, 33 * 16], F32)
            dump("sig8", sig8[:], [16, NOWN], BF16)
            dump("xnT0", xnT[:, 0, :], [128, L], BF16)
            dump("xnT7", xnT[:, 7, :], [128, L], BF16)
            dump("xnh3", xnh[:, 3, :], [128, NHALO], BF16)
            dump("Sc", Sc[:], [16, L], F32)
            S.barrier()


        KA[1] = sb(pa, "KA1", [65, L], BF16)
        KB[1] = sb(pa, "KB1", [65, L], BF16)
        QA[1] = sb(pa, "QA1", [65, NOWN], BF16)
        QB[1] = sb(pa, "QB1", [65, NOWN], BF16)
        V1[1] = sb(pa, "V1_1", [128, 33, 192], BF16)
        sg[1] = sb(pa, "sg1", [128, NOWN], BF16)
        memset_aug(1)
        for hp in range(NPAIR):
            if hp > 0:
                g_proj(hp)
            if hp + 1 < NPAIR:
                pre = [lambda hp=hp: load_g(hp + 1)]
                if 1 <= hp <= 4:
                    def wobf_cast(hp=hp):
                        for q8 in (2 * (hp - 1), 2 * (hp - 1) + 1):
                            S.dma("pool", "wobf", wo_bf[256 * q8:256 * (q8 + 1), :], w_out[256 * q8:256 * (q8 + 1), :])
                    pre.append(wobf_cast)
                attention(hp, pre, proj_chunks(hp + 1), [lambda hp=hp: load_kqv(hp + 2)] if hp + 2 < NPAIR else [])
            else:
                def pool_w_prefetch():
                    S.dma("pool", "wu0", wq[:, 0, :, :],
                          w_in[:, C_U + 128 * 6:C_U + 128 * 7].rearrange("(c p) n -> p c n", p=128), writes_nc=[wkqvb, wgb])
                    S.dma("pool", "wu0", wq[:, 1, :, :],
                          w_in[:, C_GP + 128 * 6:C_GP + 128 * 7].rearrange("(c p) n -> p c n", p=128), writes_nc=[wkqvb, wgb])
                    S.dma("pool", "wpl", wq[:, 2:4, :, :].rearrange("p a c n -> p (a c n)").rearrange("p (g c e) -> p g c e", g=4, c=2),
                          w_pool.rearrange("g (c p) e -> p g c e", p=128), writes_nc=[wkqvb, wgb])
                attention(hp, [pool_w_prefetch], [])
        S.barrier()
        S.cps.pop()
        dump("KA", KA[(NPAIR - 1) % 2][:], [65, L], BF16)
        dump("KB", KB[(NPAIR - 1) % 2][:], [65, L], BF16)
        dump("QA", QA[(NPAIR - 1) % 2][:], [65, NOWN], BF16)
        dump("QB", QB[(NPAIR - 1) % 2][:], [65, NOWN], BF16)
        dump("V1", V1[(NPAIR - 1) % 2][:].rearrange("p s c -> p (s c)"), [128, 33 * 192], BF16)
        dump("sg", sg[(NPAIR - 1) % 2][:], [128, NOWN], BF16)
        dump("mixA", mixA[:].rearrange("p a t -> p (a t)"), [128, 8 * NOWN], BF16)
        S.barrier()

    es_p = ExitStack()
    mixP = sb(es_p, "mixP", [128, 8, NOWN], BF16)
    with ExitStack() as pp:
        wu = [wq[:, 0:2, :, :], sb(pp, "wu1", [128, 2, 8, 128], BF16)]
        wub = S.bufs_n(2)
        wpl = wq[:, 2:4, :, :].rearrange("p a c n -> p (a c n)").rearrange("p (g c e) -> p g c e", g=4, c=2)
        wplb = S.buf()
        U = [sb(pp, f"U{i}", [128, 8, 144], F32) for i in range(2)]
        Ub = S.bufs_n(2)
        T1 = sb(pp, "T1", [128, 8, 144], F32)
        T2 = sb(pp, "T2", [128, 8, 144], F32)
        T1b, T2b = S.bufs_n(2)
        dT = sb(pp, "dT", [128, 2, NOWN], BF16)
        dTb = S.bufs_n(2)
        sgp = [sb(pp, f"sgp{i}", [128, NOWN], BF16) for i in range(2)]
        sgpb = S.bufs_n(2)
        pTb = S.buf()

        def load_wu(ct):
            k = ct % 2
            S.dma("pool", f"wu{k}", wu[k][:, 0, :, :],
                  w_in[:, C_U + 128 * ct:C_U + 128 * (ct + 1)].rearrange("(c p) n -> p c n", p=128), writes_nc=[wub[k]])
            S.dma("pool", f"wu{k}", wu[k][:, 1, :, :],
                  w_in[:, C_GP + 128 * ct:C_GP + 128 * (ct + 1)].rearrange("(c p) n -> p c n", p=128), writes_nc=[wub[k]])

        pc = [0]

        def pbank():
            b = pc[0] % 4
            pc[0] += 1
            return b

        hcount = 0
        pending = []
        xnTb = S.buf()
        es_w = ExitStack()
        wo_box = []
        CT_ORDER = [6, 7, 4, 5, 2, 3, 0, 1]
        for cti, ct in enumerate(CT_ORDER):
            k = ct % 2
            gi = ct // 2
            nlev = gi + 1
            w = 2 ** nlev
            if cti + 1 < 8:
                load_wu(CT_ORDER[cti + 1])
            W = wu[k]
            for hf in range(2):
                uk = hcount % 2
                hcount += 1
                Uh, Uhb = U[uk], Ub[uk]
                for t in (2 * hf, 2 * hf + 1):
                    c0 = NM + 512 * t
                    bank = pbank()
                    for c in range(8):
                        S.op("pe", lambda e, bank=bank, c=c, c0=c0, W=W: e.matmul(
                            ps[bank][:, :], lhsT=W[:, 0, c, :], rhs=xnT[:, c, c0:c0 + 512], start=(c == 0), stop=(c == 7)),
                            reads=[wub[k], xnTb], writes=[psb[bank]], sig=(c == 7))
                    b0 = 4 * (t - 2 * hf)
                    S.op("dve", lambda e, bank=bank, b0=b0, Uh=Uh: e.tensor_copy(
                        out=Uh[:, b0:b0 + 4, 16:144], in_=ps[bank][:, :].rearrange("p (s c) -> p s c", c=128)),
                        reads=[psb[bank]], writes_nc=[Uhb])
                bank = pbank()
                for c in range(8):
                    S.op("pe", lambda e, bank=bank, c=c, W=W, hf=hf: e.matmul(
                        ps[bank][:, 0:128], lhsT=W[:, 0, c, :], rhs=xnh[:, c, 128 * hf:128 * (hf + 1)], start=(c == 0), stop=(c == 7)),
                        reads=[wub[k], xnTb], writes=[psb[bank]], sig=(c == 7))
                S.op("dve", lambda e, bank=bank, Uh=Uh: e.tensor_copy(
                    out=Uh[:, :, 0:16], in_=ps[bank][:, 0:128].rearrange("p (s c) -> p s c", c=16)),
                    reads=[psb[bank]], writes_nc=[Uhb])
                if hf == 0 and pending:
                    pending.pop(0)()
                for t in (2 * hf, 2 * hf + 1):
                    c0 = NM + 512 * t
                    bank = pbank()
                    for c in range(8):
                        S.op("pe", lambda e, bank=bank, c=c, c0=c0, W=W: e.matmul(
                            ps[bank][:, :], lhsT=W[:, 1, c, :], rhs=xnT[:, c, c0:c0 + 512], start=(c == 0), stop=(c == 7)),
                            reads=[wub[k], xnTb], writes=[psb[bank]], sig=(c == 7))
                    S.op("act", lambda e, bank=bank, t=t, k=k: e.activation(out=sgp[k][:, 512 * t:512 * (t + 1)], in_=ps[bank][:, :], func=AF.Silu),
                         reads=[psb[bank]], writes_nc=[sgpb[k]])
                src, srcb = Uh, Uhb
                tmp = [(T1, T1b), (T2, T2b)]
                for lv in range(nlev):
                    sh = 2 ** lv
                    lo = 16 - (w - 2 * sh)
                    dst, dstb = tmp[lv % 2]
                    weng = "pool" if (lv + hcount) % 2 == 0 else "dve"
                    S.op(weng, lambda e, src=src, dst=dst, sh=sh, lo=lo: e.tensor_tensor(
                        out=dst[:, :, lo:144], in0=src[:, :, lo:144], in1=src[:, :, lo - sh:144 - sh], op=ALU.add),
                        reads=[srcb], writes_nc=[dstb])
                    src, srcb = dst, dstb
                S.op("dve", lambda e, src=src, k=k, w=w, hf=hf, Uh=Uh: e.scalar_tensor_tensor(
                    out=dT[:, k, 1024 * hf:1024 * (hf + 1)].rearrange("p (s c) -> p s c", c=128), in0=src[:, :, 16:144], scalar=1.0 / w,
                    in1=Uh[:, :, 16:144], op0=ALU.mult, op1=ALU.subtract),
                    reads=[srcb, Uhb], writes_nc=[dTb[k]])
                if cti == 7 and hf == 1:
                    es_x.close()
                    wo_ = sb(es_w, "wo", [128, 16, D], BF16, side="right")
                    wobs_ = S.bufs_n(4)
                    wo_box.append((wo_, wobs_))
                    for q4 in range(4):
                        S.dma("sp", f"wo{q4}", wo_[:, 4 * q4:4 * q4 + 4, :],
                              wo_bf[512 * q4:512 * (q4 + 1), :].rearrange("(c p) n -> p c n", p=128),
                              writes_nc=[xnTb, wobs_[q4]])
            if k == 1:
                def wpool_emit(gi=gi):
                    for et_ in range(2):
                        ctp = 2 * gi + et_
                        for t in range(4):
                            bank = 4 + (pc[0] % 4)
                            pc[0] += 1
                            for cc in range(2):
                                S.op("pe", lambda e, bank=bank, cc=cc, gi=gi, et_=et_, t=t: e.matmul(
                                    ps[bank][:, :], lhsT=wpl[:, gi, cc, 128 * et_:128 * (et_ + 1)],
                                    rhs=dT[:, cc, 512 * t:512 * (t + 1)], start=(cc == 0), stop=(cc == 1)),
                                    reads=[wplb, dTb[0], dTb[1]], writes=[psb[bank]], sig=(cc == 1))
                            S.op("dve", lambda e, bank=bank, ctp=ctp, et_=et_, t=t: e.scalar_tensor_tensor(
                                out=mixP[:, ctp, 512 * t:512 * (t + 1)], in0=ps[bank][:, :], scalar=cv[:, 16 + ctp:17 + ctp],
                                in1=sgp[et_][:, 512 * t:512 * (t + 1)], op0=ALU.mult, op1=ALU.mult),
                                reads=[psb[bank], sgpb[et_]], writes_nc=[pTb])
                pending.append(wpool_emit)
        while pending:
            pending.pop(0)()
        S.barrier(skip_slots=("wo0", "wo1", "wo2", "wo3"), keep_bufs=wo_box[0][1])
    wo, wobs = wo_box[0]

    with ExitStack() as po:
        lnb = sb(po, "lnb", [128, 4, D], F32)
        lnbb = S.buf()
        NX = 3
        o_xt = [sb(po, f"oxt{i}", [128, D], F32) for i in range(NX)]
        o_xtb = S.bufs_n(NX)
        x0 = [sb(po, f"x0{i}", [128, D], F32) for i in range(NX)]
        x0b = S.bufs_n(NX)
        zt = [sb(po, f"zt{i}", [128, D], F32) for i in range(2)]
        ztb = S.bufs_n(2)
        ot = [sb(po, f"ot{i}", [128, D], F32) for i in range(2)]
        otb = S.bufs_n(2)
        NS = 6
        o_st = [sb(po, f"ost{i}", [128, 2, 6], F32) for i in range(NS)]
        o_mv = [sb(po, f"omv{i}", [128, 2], F32) for i in range(NS)]
        o_sd = [sb(po, f"osd{i}", [128, 1], F32) for i in range(NS)]
        o_rs = [sb(po, f"ors{i}", [128, 1], F32) for i in range(NS)]
        o_stb, o_mvb, o_sdb, o_rsb = S.bufs_n(NS), S.bufs_n(NS), S.bufs_n(NS), S.bufs_n(NS)
        for i4 in range(4):
            S.dma("sp", "lnb", lnb[:, i4, :], lnrows_d[i4:i4 + 1, :].partition_broadcast(128)[:, 0, :], writes_nc=[lnbb])
        S.op("pool", lambda e: e.tensor_scalar(out=lnb[:, 0:2, :], in0=lnb[:, 0:2, :], scalar1=ALPHA, scalar2=None, op0=ALU.mult),
             reads=[lnbb], writes=[lnbb])

        def ln_stats(kk, src):
            S.op("dve", lambda e: e.bn_stats(out=o_st[kk][:, 0, :], in_=src[0][:, 0:512]), reads=[src[1]], writes=[o_stb[kk]])
            S.op("dve", lambda e: e.bn_stats(out=o_st[kk][:, 1, :], in_=src[0][:, 512:1024]), reads=[src[1]], writes_nc=[o_stb[kk]])
            S.op("dve", lambda e: e.bn_aggr(out=o_mv[kk][:, :], in_=o_st[kk][:].rearrange("p a b -> p (a b)")),
                 reads=[o_stb[kk]], writes=[o_mvb[kk]])
            S.op("act", lambda e: e.activation(out=o_sd[kk][:, :], in_=o_mv[kk][:, 1:2], func=AF.Sqrt, bias=epsT[:, 0:1], scale=1.0),
                 reads=[o_mvb[kk]], writes=[o_sdb[kk]])

        def ln_stats_b(kk):
            S.op("dve", lambda e: e.reciprocal(out=o_rs[kk][:, :], in_=o_sd[kk][:, :]), reads=[o_sdb[kk]], writes=[o_rsb[kk]])

        def prep(i):
            k = i % NX
            kk = i % 3
            S.dma("sp", f"oxt{k}", o_xt[k][:, :], xs[NM + 128 * i:NM + 128 * (i + 1), :], writes=[o_xtb[k]])
            S.op("dve", lambda e, k=k, i=i: e.tensor_scalar(
                out=x0[k][:, :], in0=o_xt[k][:, :], scalar1=mv_all[:, 1 + i, 0:1], scalar2=rs_all[:, 1 + i:2 + i],
                op0=ALU.subtract, op1=ALU.mult), reads=[o_xtb[k]], writes=[x0b[k]])
            S.op("pool", lambda e, k=k: e.tensor_tensor(out=x0[k][:, :], in0=x0[k][:, :], in1=lnb[:, 0, :], op=ALU.mult),
                 reads=[x0b[k], lnbb], writes=[x0b[k]])
            S.op("pool", lambda e, k=k: e.tensor_tensor(out=x0[k][:, :], in0=x0[k][:, :], in1=lnb[:, 1, :], op=ALU.add),
                 reads=[x0b[k], lnbb], writes=[x0b[k]])

        prep(0)
        prep(1)
        for i in range(16):
            k = i % NX
            k2 = i % 2
            for half in range(2):
                bank = (2 * i + half) % 8
                for c in range(16):
                    S.op("pe", lambda e, bank=bank, c=c, i=i, half=half: e.matmul(
                        ps[bank][:, :], lhsT=(mixA if c < 8 else mixP)[:, c % 8, 128 * i:128 * (i + 1)],
                        rhs=wo[:, c, 512 * half:512 * (half + 1)],
                        start=(c == 0), stop=(c == 15)), reads=[wobs[c // 4]], writes=[psb[bank]], sig=(c == 15))
                S.op("dve", lambda e, bank=bank, k=k, k2=k2, half=half: e.tensor_tensor(
                    out=zt[k2][:, 512 * half:512 * (half + 1)], in0=ps[bank][:, :], in1=x0[k][:, 512 * half:512 * (half + 1)],
                    op=ALU.add), reads=[psb[bank], x0b[k]], writes_nc=[ztb[k2]])
            kk = 3 + (i % 3)
            ln_stats(kk, (zt[k2], ztb[k2]))
            if i + 2 < 16:
                prep(i + 2)
            ln_stats_b(kk)
            S.op("dve", lambda e, k2=k2, kk=kk: e.scalar_tensor_tensor(
                out=ot[k2][:, :], in0=zt[k2][:, :], scalar=o_mv[kk][:, 0:1], in1=lnb[:, 2, :],
                op0=ALU.subtract, op1=ALU.mult), reads=[ztb[k2], o_mvb[kk], lnbb], writes=[otb[k2]])
            S.op("dve", lambda e, k2=k2, kk=kk: e.scalar_tensor_tensor(
                out=ot[k2][:, :], in0=ot[k2][:, :], scalar=o_rs[kk][:, 0:1], in1=lnb[:, 3, :],
                op0=ALU.mult, op1=ALU.add), reads=[otb[k2], o_rsb[kk], lnbb], writes=[otb[k2]])
            S.dma("act", f"oy{k2}", y[128 * i:128 * (i + 1), :], ot[k2][:, :], reads=[otb[k2]])
        S.barrier()

    with ExitStack() as semstack:
        sems = {}
        for e in Sched.ENG:
            sems[("E", e)] = semstack.enter_context(nc.semaphore(f"sem_{e}"))
        for slot in S.dcount:
            sems[("D", slot)] = semstack.enter_context(nc.semaphore(f"semd_{slot}"))
        with nc.Block() as block:
            S.replay(nc, block, sems, upto)
    es_w.close()
    es_p.close()
    es.close()
    return nc


def _core_inputs(c, x, meta_tokens, shared):
    b, r = c // 2, c % 2
    hfull = np.concatenate([meta_tokens, x[b]], axis=0)
    own_g = [2 * i + r for i in range(16)]
    oth_g = [2 * j + 1 - r for j in range(16)]
    xb = x[b].reshape(32, 128, D)
    halo = np.stack([hfull[128 * g:128 * g + 16] for g in own_g], axis=0).reshape(NHALO, D)
    xs = np.concatenate([meta_tokens, xb[own_g].reshape(-1, D), xb[oth_g].reshape(-1, D), halo], axis=0)
    gl = np.array(own_g + oth_g)
    sl = np.arange(32)
    mm = (gl[:, None] < gl[None, :]).astype(np.float32) - (sl[:, None] < sl[None, :]).astype(np.float32)
    kq = np.arange(128)
    tri = np.where(kq[:, None] <= kq[None, :], 0.0, MASKV).astype(np.float32)
    mx = np.full((128, 128), MASKV if r == 0 else 0.0, np.float32)
    d = dict(shared)
    d["xs"] = np.ascontiguousarray(xs, dtype=np.float32)
    d["mmat"] = np.ascontiguousarray(mm)
    d["masks"] = np.ascontiguousarray(np.concatenate([tri, mx], axis=1))
    return d, own_g


def kernel(x, meta_tokens, ln_in_g, ln_in_b, w_in, b_forget, w_pool, pool_scale, w_out, ln_g, ln_b):
    x = np.asarray(x, np.float32)
    meta_tokens = np.asarray(meta_tokens, np.float32)
    f = lambda a: np.ascontiguousarray(np.asarray(a, np.float32))
    cvec = np.concatenate([f(ln_in_g).reshape(8, 128).T, f(ln_in_b).reshape(8, 128).T,
                           f(pool_scale)[0].reshape(8, 128).T], axis=1)
    shared = {
        "w_in": f(w_in)[0], "w_pool": f(w_pool)[0], "w_out": f(w_out)[0],
        "cvec": np.ascontiguousarray(cvec), "bfv": f(b_forget)[0].reshape(16, 1),
        "ident": np.eye(128, dtype=np.float32),
        "lnrows": np.ascontiguousarray(np.stack([f(ln_in_g), f(ln_in_b), f(ln_g)[0], f(ln_b)[0]], axis=0)),
    }
    in_maps, owns = [], []
    for c in range(8):
        d, own_g = _core_inputs(c, x, meta_tokens, shared)
        in_maps.append(d)
        owns.append(own_g)
    nc = build_nc()
    res = run_bass_kernel_spmd(nc, in_maps, core_ids=list(range(8)))
    out = np.empty((4, SEQ, D), np.float32)
    for c in range(8):
        yb = np.asarray(res.results[c]["y"], np.float32).reshape(16, 128, D)
        ov = out[c // 2].reshape(32, 128, D)
        ov[owns[c]] = yb
    return out
```

```python
import numpy as np
from contextlib import ExitStack
import concourse.bass as bass
import concourse.mybir as mybir
from concourse.bass_utils import run_bass_kernel_spmd

F32 = mybir.dt.float32
BF16 = mybir.dt.bfloat16
AF = mybir.ActivationFunctionType
ALU = mybir.AluOpType

D = 1024
SEQ = 4096
NM = 16
L = SEQ + NM
H = 16
NOWN = 2048
NHALO = 256
ROWS = NM + SEQ + NHALO
INCOLS = 6160
C_Q, C_K, C_V, C_F, C_G, C_U, C_GP = 0, 1024, 2048, 3072, 3088, 4112, 5136
ALPHA = 2.0 ** 0.25
EPS = 1e-5
MASKV = -30000.0
LA = 2


class Buf:
    __slots__ = ("w", "r")

    def __init__(self):
        self.w = {}
        self.r = {}


class Sched:
    ENG = ("pe", "act", "dve", "pool", "sp")

    def __init__(self):
        self.q = {n: [] for n in self.ENG}
        self.dcount = {}
        self.bufs = []
        self.cps = []

    def buf(self):
        b = Buf()
        self.bufs.append(b)
        return b

    def bufs_n(self, n):
        return [self.buf() for _ in range(n)]

    def _deps(self, eng, reads, writes, writes_nc):
        waits = {}

        def add(k, v, war=False):
            if k[0] == "E" and k[1] == eng and eng == "pe":
                return
            if waits.get(k, -1) < v:
                waits[k] = v

        for b in reads:
            for k, v in b.w.items():
                add(k, v)
        for b in writes:
            for k, v in b.w.items():
                add(k, v)
            for k, v in b.r.items():
                add(k, v, war=True)
        for b in writes_nc:
            for k, v in b.r.items():
                add(k, v, war=True)
        return waits

    @staticmethod
    def _mark(tok, reads, writes, writes_nc):
        k = (tok[0], tok[1])
        for b in reads:
            if b.r.get(k, -1) < tok[2]:
                b.r[k] = tok[2]
        for b in writes:
            b.w = {k: tok[2]}
            b.r = {}
        for b in writes_nc:
            if b.w.get(k, -1) < tok[2]:
                b.w[k] = tok[2]

    def op(self, eng, fn, reads=(), writes=(), writes_nc=(), sig=True):
        waits = self._deps(eng, reads, writes, writes_nc)
        tok = ("E", eng, len(self.q[eng]))
        self.q[eng].append(dict(kind="op", fn=fn, waits=waits, sig=sig))
        self._mark(tok, reads, writes, writes_nc)
        return tok

    def dma(self, eng, slot, out, in_, reads=(), writes=(), writes_nc=(), **kw):
        waits = self._deps(eng, reads, writes, writes_nc)
        self.dcount[slot] = self.dcount.get(slot, 0) + 16
        tok = ("D", slot, self.dcount[slot])
        self.q[eng].append(dict(kind="dma", out=out, in_=in_, waits=waits, slot=slot, sig=False, kw=kw))
        self._mark(tok, reads, writes, writes_nc)
        return tok

    def barrier(self, skip_slots=(), keep_bufs=()):
        toks = []
        for e in self.ENG:
            ops = self.q[e]
            last = None
            for i in range(len(ops) - 1, -1, -1):
                if ops[i]["kind"] == "op":
                    last = i
                    break
            if last is not None:
                ops[last]["sig"] = True
                toks.append(("E", e, last))
        for slot, v in self.dcount.items():
            if slot not in skip_slots:
                toks.append(("D", slot, v))
        for e in self.ENG:
            waits = {}
            for t in toks:
                if t[0] == "E" and t[1] == e:
                    continue
                waits[(t[0], t[1])] = t[2]
            self.q[e].append(dict(kind="wait", waits=waits, sig=False))
        for b in self.bufs:
            if any(b is kb for kb in keep_bufs):
                continue
            b.w = {}
            b.r = {}
        self.cps.append({e: len(self.q[e]) for e in self.ENG})

    def replay(self, nc, block, sems, upto=None):
        if upto is not None:
            for e in self.ENG:
                self.q[e] = self.q[e][:self.cps[upto][e]]
        vals = {}
        for e, ops in self.q.items():
            cnt = 0
            for o in ops:
                if o["kind"] == "op" and o["sig"]:
                    cnt += 1
                o["cum"] = cnt
            nxt = None
            v = [None] * len(ops)
            for i in range(len(ops) - 1, -1, -1):
                if ops[i]["kind"] == "op" and ops[i]["sig"]:
                    nxt = ops[i]["cum"]
                v[i] = nxt
            vals[e] = v

        def run(e, handle):
            waited = {}
            own = sems[("E", e)]
            for o in self.q[e]:
                for key, val in o["waits"].items():
                    if key[0] == "E":
                        val = vals[key[1]][val]
                        assert val is not None
                    if waited.get(key, -1) >= val:
                        continue
                    waited[key] = val
                    handle.wait_ge(sems[key], val)
                if o["kind"] == "op":
                    ins = o["fn"](handle)
                    if o["sig"]:
                        ins.then_inc(own, 1)
                elif o["kind"] == "dma":
                    handle.dma_start(out=o["out"], in_=o["in_"], **o["kw"]).then_inc(sems[("D", o["slot"])], 16)

        block.tensor(lambda h: run("pe", h))
        block.scalar(lambda h: run("act", h))
        block.vector(lambda h: run("dve", h))
        block.gpsimd(lambda h: run("pool", h))
        block.sync(lambda h: run("sp", h))


def build_nc(upto=None):
    nc = bass.Bass("TRN2", target_bir_lowering=False)
    S = Sched()

    def din(name, shape):
        return nc.dram_tensor(name, shape, F32, kind="ExternalInput").ap()

    xs = din("xs", [ROWS, D])
    w_in = din("w_in", [D, INCOLS])
    w_pool = din("w_pool", [4, 256, 256])
    w_out = din("w_out", [2048, D])
    cvec_d = din("cvec", [128, 24])
    bfv_d = din("bfv", [16, 1])
    mmat_d = din("mmat", [32, 32])
    ident_d = din("ident", [128, 128])
    masks_d = din("masks", [128, 256])
    lnrows_d = din("lnrows", [4, D])
    y = nc.dram_tensor("y", [NOWN, D], F32, kind="ExternalOutput").ap()
    wo_bf = nc.dram_tensor("wo_bf", [2048, D], BF16, kind="Internal").ap()

    DUMP = False

    def dump(name, ap, shape, dt):
        if not DUMP:
            return
        o = nc.dram_tensor("dbg_" + name, list(shape), dt, kind="ExternalOutput").ap()
        S.dma("sp", "dbg", o, ap)
        S.barrier()
        S.cps.pop()

    es = ExitStack()

    def sb(stack, name, shape, dt, side=None):
        if side is None:
            return stack.enter_context(nc.sbuf_tensor(name, shape, dt))
        return stack.enter_context(nc.sbuf_tensor(name, shape, dt, side=side))

    ps = [es.enter_context(nc.psum_tensor(f"ps{i}", [128, 512], F32)) for i in range(8)]
    psb = S.bufs_n(8)

    mixA = sb(es, "mixA", [128, 8, NOWN], BF16)
    ident_f = sb(es, "ident_f", [128, 128], F32)
    ident_b = sb(es, "ident_b", [128, 128], BF16)
    masks_b = sb(es, "masks_b", [128, 256], BF16)
    cv = sb(es, "cv", [128, 24], F32)
    epsT = sb(es, "epsT", [128, 1], F32)
    oneT = sb(es, "oneT", [128, 1], F32)
    nbf = sb(es, "nbf", [16, 1], F32)
    mmat = sb(es, "mmat_s", [32, 32], F32)
    negc_tok = sb(es, "negc_tok", [128, 33, 16], F32)
    sig8r = sb(es, "sig8r", [128, 256], BF16)
    constb = S.buf()
    wq = sb(es, "wqkvg", [128, 4, 8, 128], BF16)
    NT0 = 35
    mv_all = sb(es, "mv_all", [128, NT0, 2], F32)
    sd_all = sb(es, "sd_all", [128, NT0], F32)
    rs_all = sb(es, "rs_all", [128, NT0], F32)
    s1_all = sb(es, "s1_all", [128, NT0], F32)
    s2_all = sb(es, "s2_all", [128, NT0], F32)
    t1_all = sb(es, "t1_all", [128, NT0], F32)

    S.dma("sp", "const", ident_f[:], ident_d[:, :], writes=[])
    S.dma("sp", "const", cv[:], cvec_d[:, :], writes=[])
    S.dma("sp", "const", nbf[:], bfv_d[:, :], writes=[])
    S.dma("sp", "const", mmat[:], mmat_d[:, :], writes=[])
    S.dma("pool", "constp", ident_b[:], ident_d[:, :], writes=[])
    S.dma("pool", "constp", masks_b[:], masks_d[:, :], writes=[])
    S.op("dve", lambda e: e.memset(negc_tok[:, 0, :], MASKV))
    S.op("dve", lambda e: e.memset(epsT[:], EPS))
    S.op("dve", lambda e: e.memset(oneT[:], 1.0))
    S.barrier()
    S.op("dve", lambda e: e.tensor_scalar(out=nbf[:], in0=nbf[:], scalar1=-1.0, scalar2=None, op0=ALU.mult))

    es_x = ExitStack()
    xnT = sb(es_x, "xnT", [128, 8, L], BF16, side="right")
    xnh = sb(es_x, "xnh", [128, 8, NHALO], BF16, side="right")

    for gi_w, cbase_w in enumerate((C_Q, C_K, C_V, C_G)):
        S.dma("pool", "wq", wq[:, gi_w, :, :],
              w_in[:, cbase_w:cbase_w + 128].rearrange("(c p) n -> p c n", p=128))

    with ExitStack() as p0:
        NXT = 6
        NXH = 8
        xt = [sb(p0, f"xt{i}", [128, D], F32) for i in range(NXT)]
        xtb = S.bufs_n(NXT)
        xh = [sb(p0, f"xh{i}", [128, D], BF16) for i in range(NXH)]
        psbf = [ps[i][:, :].bitcast(BF16) for i in range(4)]
        xhb = S.bufs_n(NXH)
        st = [sb(p0, f"st{i}", [128, 2, 6], F32) for i in range(NXT)]
        stb = S.bufs_n(NXT)
        mvb, sdb, rsb = S.bufs_n(NT0), S.bufs_n(NT0), S.bufs_n(NT0)

        groups = [[(0, NM, xnT, 0)]]
        for g4 in range(8):
            groups.append([(NM + 128 * s, 128, xnT, NM + 128 * s) for s in range(4 * g4, 4 * g4 + 4)])
        groups.append([(NM + SEQ, 128, xnh, 0), (NM + SEQ + 128, 128, xnh, 128)])
        flat = []
        for gi_, grp in enumerate(groups):
            for j_, (r0, P, dst, c0) in enumerate(grp):
                flat.append((gi_, j_, r0, P))
        bcount = [0]

        junk2 = [sb(p0, f"junk{i}", [128, D], BF16) for i in range(2)]
        junkb2 = S.bufs_n(2)
        s1b, s2b, t1b = S.bufs_n(NT0), S.bufs_n(NT0), S.bufs_n(NT0)

        def stageA1(ti):
            gi_, j_, r0, P = flat[ti]
            k = ti % NXT
            tt_ = ti
            S.dma("sp", f"xt{k}", xt[k][0:P, :], xs[r0:r0 + P, :], writes=[xtb[k]])
            S.op("act", lambda e: e.activation(out=junk2[0][0:P, :], in_=xt[k][0:P, :], func=AF.Identity,
                                               accum_out=s1_all[0:P, tt_:tt_ + 1]),
                 reads=[xtb[k]], writes=[s1b[tt_], junkb2[0]])
            S.op("act", lambda e: e.activation(out=junk2[1][0:P, :], in_=xt[k][0:P, :], func=AF.Square,
                                               accum_out=s2_all[0:P, tt_:tt_ + 1]),
                 reads=[xtb[k]], writes=[s2b[tt_], junkb2[1]])

        def stageA2(ti):
            gi_, j_, r0, P = flat[ti]
            tt_ = ti
            S.op("dve", lambda e: e.tensor_scalar(out=mv_all[0:P, tt_, 0:1], in0=s1_all[0:P, tt_:tt_ + 1], scalar1=1.0 / D,
                                                  scalar2=None, op0=ALU.mult), reads=[s1b[tt_]], writes=[mvb[tt_]])
            S.op("dve", lambda e: e.tensor_tensor(out=t1_all[0:P, tt_:tt_ + 1], in0=mv_all[0:P, tt_, 0:1], in1=mv_all[0:P, tt_, 0:1],
                                                  op=ALU.mult), reads=[mvb[tt_]], writes=[t1b[tt_]])
            S.op("dve", lambda e: e.scalar_tensor_tensor(out=mv_all[0:P, tt_, 1:2], in0=s2_all[0:P, tt_:tt_ + 1], scalar=1.0 / D,
                                                         in1=t1_all[0:P, tt_:tt_ + 1], op0=ALU.mult, op1=ALU.subtract),
                 reads=[s2b[tt_], t1b[tt_]], writes_nc=[mvb[tt_]])
            S.op("act", lambda e: e.activation(out=sd_all[0:P, tt_:tt_ + 1], in_=mv_all[0:P, tt_, 1:2], func=AF.Sqrt,
                                               bias=epsT[0:P, 0:1], scale=1.0),
                 reads=[mvb[tt_]], writes=[sdb[tt_]])

        def stageB(ti):
            gi_, j_, r0, P = flat[ti]
            k = ti % NXT
            kh = ti % NXH
            tt_ = ti
            S.op("dve", lambda e: e.reciprocal(out=rs_all[0:P, tt_:tt_ + 1], in_=sd_all[0:P, tt_:tt_ + 1]),
                 reads=[sdb[tt_]], writes=[rsb[tt_]])
            S.op("dve", lambda e: e.tensor_scalar(
                out=xh[kh][0:P, :], in0=xt[k][0:P, :], scalar1=mv_all[0:P, tt_, 0:1], scalar2=rs_all[0:P, tt_:tt_ + 1],
                op0=ALU.subtract, op1=ALU.mult),
                reads=[xtb[k], mvb[tt_], rsb[tt_]], writes=[xhb[kh]])

        def stageC(gi_, ti_last):
            grp = groups[gi_]
            ti0 = ti_last - len(grp) + 1
            dst, c00 = grp[0][2], grp[0][3]
            ncols = sum(P for (_, P, _, _) in grp)
            for c in range(8):
                bank = bcount[0] % 4
                bcount[0] += 1
                off = 0
                for j, (r0, P, _, _) in enumerate(grp):
                    kh = (ti0 + j) % NXH
                    S.op("pe", lambda e, bank=bank, off=off, c=c, kh=kh, P=P: e.transpose(
                        out=psbf[bank][:, off:off + P], in_=xh[kh][0:P, c * 128:(c + 1) * 128],
                        identity=ident_b[0:P, 0:P]),
                        reads=[xhb[kh]], writes=[psb[bank]], sig=(j == len(grp) - 1))
                    off += P
                S.op("dve", lambda e, bank=bank, c=c, dst=dst, c00=c00, ncols=ncols: e.tensor_scalar(
                    out=dst[:, c, c00:c00 + ncols], in0=psbf[bank][:, 0:ncols],
                    scalar1=cv[:, c:c + 1], scalar2=cv[:, 8 + c:9 + c], op0=ALU.mult, op1=ALU.add),
                    reads=[psb[bank]], writes=[])

        nflat = len(flat)
        stageA1(0)
        stageA1(1)
        stageA1(2)
        stageA2(0)
        for ti in range(nflat):
            if ti + 3 < nflat:
                stageA1(ti + 3)
            if ti + 1 < nflat:
                stageA2(ti + 1)
            stageB(ti)
            gi_, j_, _, _ = flat[ti]
            if j_ == len(groups[gi_]) - 1:
                stageC(gi_, ti)
        S.barrier()

    with ExitStack() as pa:
        wkqvb, wgb = S.bufs_n(2)
        KA = [sb(pa, "KA0", [65, L], BF16), None]
        KB = [sb(pa, "KB0", [65, L], BF16), None]
        QA = [sb(pa, "QA0", [65, NOWN], BF16), None]
        QB = [sb(pa, "QB0", [65, NOWN], BF16), None]
        V1 = [sb(pa, "V1_0", [128, 33, 192], BF16), None]
        sg = [sb(pa, "sg0", [128, NOWN], BF16), None]
        sig8rb = S.buf()
        Kb, Qb, Vb, sgb = S.bufs_n(2), S.bufs_n(2), S.bufs_n(2), S.bufs_n(2)
        aTb = S.buf()
        NP = 3
        Pt = [sb(pa, f"Pt{i}", [128, 512], BF16) for i in range(NP)]
        Ptb = S.bufs_n(NP)
        recb = sb(pa, "recb0", [128, 512], F32)
        recbb = S.buf()
        tt = sb(pa, "tt0", [128, 512], F32)
        ttb = S.buf()

        Vzb = S.bufs_n(2)

        def memset_aug(i2):
            S.op("pool", lambda e: e.memset(KA[i2][64:65, :], 1.0), writes_nc=[Kb[i2]])
            S.op("pool", lambda e: e.memset(KB[i2][64:65, :], 1.0), writes_nc=[Kb[i2]])
            S.op("pool", lambda e: e.memset(V1[i2][:, :, 64:128], 1.0), writes_nc=[Vb[i2]])
            S.op("pool", lambda e: e.memset(V1[i2][:, 0, 0:64], 0.0), writes_nc=[Vb[i2], Vzb[i2]])
            S.op("pool", lambda e: e.memset(V1[i2][:, 0, 128:192], 0.0), writes_nc=[Vb[i2], Vzb[i2]])
        memset_aug(0)

        def load_kqv(hp):
            for gi, cbase in enumerate((C_Q, C_K, C_V)):
                S.dma("pool", "wq", wq[:, gi, :, :],
                      w_in[:, cbase + 128 * hp:cbase + 128 * (hp + 1)].rearrange("(c p) n -> p c n", p=128),
                      writes_nc=[wkqvb])

        def load_g(hp):
            S.dma("pool", "wg", wq[:, 3, :, :],
                  w_in[:, C_G + 128 * hp:C_G + 128 * (hp + 1)].rearrange("(c p) n -> p c n", p=128),
                  writes_nc=[wgb])

        SB = [2, 3, 4]
        OB = [5, 6]
        PB = [0, 1, 7]
        LA_ = 2
        pcount = [0]
        Wpair = wq

        def proj_bank():
            bk = PB[pcount[0] % len(PB)]
            pcount[0] += 1
            return bk

        def mm8(gi, rhs_of, N, state, lo_c, hi_c):
            bank = state["bank"]
            for c in range(lo_c, hi_c):
                S.op("pe", lambda e, bank=bank, c=c: e.matmul(
                    ps[bank][:, 0:N], lhsT=Wpair[:, gi, c, :], rhs=rhs_of(c), start=(c == 0), stop=(c == 7)),
                    reads=[wkqvb if gi < 3 else wgb], writes=[psb[bank]], sig=(c == 7))

        def proj_chunks(hp):
            k = hp % 2
            out = []

            def sig_dma():
                S.dma("pool", f"sigA{k}", QA[k][64:65, :].rearrange("o (a b) -> o a b", b=256), sig8r[16 * hp:16 * hp + 8, :], reads=[sig8rb], writes_nc=[Qb[k]])
                S.dma("pool", f"sigB{k}", QB[k][64:65, :].rearrange("o (a b) -> o a b", b=256), sig8r[16 * hp + 8:16 * hp + 16, :], reads=[sig8rb], writes_nc=[Qb[k]])
            out.append(sig_dma)
            jobs = []
            for t in range((L + 511) // 512):
                c0 = 512 * t
                jobs.append((1, c0, min(512, L - c0), KA[k], KB[k], c0, Kb[k]))
            for t in range(4):
                jobs.append((0, NM + 512 * t, 512, QA[k], QB[k], 512 * t, Qb[k]))
            for (gi, c0, N, TA, TB, d0, tb) in jobs:
                state = {}

                def first(gi=gi, c0=c0, N=N, state=state):
                    state["bank"] = proj_bank()
                    mm8(gi, lambda c: xnT[:, c, c0:c0 + N], N, state, 0, 4)

                def second(gi=gi, c0=c0, N=N, TA=TA, TB=TB, d0=d0, tb=tb, state=state):
                    mm8(gi, lambda c: xnT[:, c, c0:c0 + N], N, state, 4, 8)
                    bank = state["bank"]
                    S.op("dve", lambda e: e.tensor_copy(out=TA[0:64, d0:d0 + N], in_=ps[bank][0:64, 0:N]),
                         reads=[psb[bank]], writes_nc=[tb])
                    S.op("dve", lambda e: e.tensor_copy(out=TB[0:64, d0:d0 + N], in_=ps[bank][64:128, 0:N]),
                         reads=[psb[bank]], writes_nc=[tb])
                out.append(first)
                out.append(second)
            vblocks = [(0, NM, 0)] + [(NM + 128 * s, 128, 1 + s) for s in range(32)]
            for g0 in range(0, 33, 4):
                grp = vblocks[g0:g0 + 4]
                state = {}
                for j, (c0, P, slot) in enumerate(grp):
                    def vblk(j=j, c0=c0, P=P, grp=grp, g0=g0, state=state):
                        if j == 0:
                            state["bank"] = proj_bank()
                        bank = state["bank"]
                        lastj = (j == len(grp) - 1)
                        for c in range(8):
                            S.op("pe", lambda e, c=c: e.matmul(
                                ps[bank][0:P, 128 * j:128 * (j + 1)], lhsT=xnT[:, c, c0:c0 + P], rhs=Wpair[:, 2, c, :],
                                start=(c == 0), stop=(c == 7)),
                                reads=[wkqvb], writes=[psb[bank]], sig=(c == 7 and lastj))
                        if not lastj:
                            return
                        if g0 == 0:
                            S.op("dve", lambda e: e.tensor_copy(out=V1[k][0:NM, 0, 0:64], in_=ps[bank][0:NM, 0:64]),
                                 reads=[psb[bank], Vzb[k]], writes_nc=[Vb[k]])
                            S.op("dve", lambda e: e.tensor_copy(out=V1[k][0:NM, 0, 128:192], in_=ps[bank][0:NM, 64:128]),
                                 reads=[psb[bank], Vzb[k]], writes_nc=[Vb[k]])
                            lo = 1
                        else:
                            lo = 0
                        n = len(grp) - lo
                        if n > 0:
                            s0 = grp[lo][2]
                            for (vo, po_) in ((0, 0), (128, 64)):
                                S.op("dve", lambda e, vo=vo, po_=po_: e.tensor_copy(
                                    out=V1[k][:, s0:s0 + n, vo:vo + 64],
                                    in_=ps[bank][:, 128 * lo:128 * (lo + n)].rearrange("p (s c) -> p s c", c=128)[:, :, po_:po_ + 64]),
                                    reads=[psb[bank]], writes_nc=[Vb[k]])
                    out.append(vblk)
            return out

        def g_proj(hp):
            k = hp % 2
            for t in range(4):
                c0 = NM + 512 * t
                state = {"bank": proj_bank()}
                mm8(3, lambda c, c0=c0: xnT[:, c, c0:c0 + 512], 512, state, 0, 8)
                bank = state["bank"]
                S.op("act", lambda e, bank=bank, t=t, k=k: e.activation(out=sg[k][:, 512 * t:512 * (t + 1)], in_=ps[bank][:, :], func=AF.Silu),
                     reads=[psb[bank]], writes_nc=[sgb[k]])

        def attention(hp, pre, inserts, post=()):
            k = hp % 2
            tl = []
            for hh in range(2):
                for J in range(4):
                    lst = [(128, 0, 0, 0, 512, None)]
                    for i in range(4 * J):
                        lst.append((128, NM + 128 * i, 1 + i, 0, 512, None))
                    for j in range(4 * J):
                        lst.append((128, NM + NOWN + 128 * j, 17 + j, 0, 512, None))
                    for sp_ in range(4):
                        i = 4 * J + sp_
                        lst.append((128, NM + 128 * i, 1 + i, 128 * sp_, 512 - 128 * sp_, 0))
                        lst.append((128, NM + NOWN + 128 * i, 17 + i, 128 * sp_, 512 - 128 * sp_, 1))
                    for n_, it in enumerate(lst):
                        tl.append(((hh, J), it, n_ == 0, n_ == len(lst) - 1))
            nt = len(tl)
            START = 4
            post = list(post)
            ins_i = 0
            obank_of = {}
            for idx in range(nt + LA_):
                if idx < nt:
                    (hh, J), (M, kc0, slot, qoff, N, mask), first, last = tl[idx]
                    Kt = KA[k] if hh == 0 else KB[k]
                    Qt = QA[k] if hh == 0 else QB[k]
                    sbank = SB[idx % len(SB)]
                    q0 = 512 * J + qoff
                    S.op("pe", lambda e, sbank=sbank, M=M, kc0=kc0, q0=q0, N=N, Kt=Kt, Qt=Qt, mask=mask: e.matmul(
                        ps[sbank][0:M, 0:N], lhsT=Kt[0:65, kc0:kc0 + M], rhs=Qt[0:65, q0:q0 + N], start=True,
                        stop=(mask is None)),
                        reads=[Kb[k], Qb[k]], writes=[psb[sbank]], sig=(mask is None))
                    if mask is not None:
                        S.op("pe", lambda e, sbank=sbank, mask=mask: e.matmul(
                            ps[sbank][:, 0:128], lhsT=ident_b[:, :], rhs=masks_b[:, 128 * mask:128 * (mask + 1)],
                            start=False, stop=True),
                            reads=[], writes=[psb[sbank]], sig=True)
                    if idx == 0:
                        for f_ in pre:
                            f_()
                    if idx >= START and ins_i < len(inserts):
                        want = ((idx - START + 1) * len(inserts) + (nt - 4 - START) - 1) // max(1, nt - 4 - START)
                        while ins_i < min(want, len(inserts)):
                            inserts[ins_i]()
                            ins_i += 1
                        if ins_i == len(inserts):
                            while post:
                                post.pop(0)()
                j = idx - LA_
                if j >= 0:
                    (hh, J), (M, kc0, slot, qoff, N, mask), first, last = tl[j]
                    sbank = SB[j % len(SB)]
                    pk = j % NP
                    h = 2 * hp + hh
                    if first:
                        obank_of[(hh, J)] = OB[ogroup[0] % 2]
                        ogroup[0] += 1
                    ob = obank_of[(hh, J)]
                    S.op("act", lambda e, sbank=sbank, pk=pk, M=M, N=N, slot=slot, h=h: e.activation(
                        out=Pt[pk][0:M, 0:N], in_=ps[sbank][0:M, 0:N], func=AF.Exp,
                        bias=negc_tok[0:M, slot, h:h + 1], scale=0.125),
                        reads=[psb[sbank]], writes=[Ptb[pk]])
                    vc0 = 0 if hh == 0 else 64
                    S.op("pe", lambda e, ob=ob, pk=pk, M=M, N=N, slot=slot, vc0=vc0, qoff=qoff, first=first, last=last: e.matmul(
                        ps[ob][:, qoff:qoff + N], lhsT=V1[k][0:M, slot, vc0:vc0 + 128], rhs=Pt[pk][0:M, 0:N],
                        start=first, stop=last),
                        reads=[Ptb[pk], Vb[k]], writes=[psb[ob]], sig=(last or idx >= nt - 1))
                    if last:
                        dr = slice(64, 128) if hh == 0 else slice(0, 64)
                        vr = slice(0, 64) if hh == 0 else slice(64, 128)
                        S.op("dve", lambda e, ob=ob, dr=dr: e.reciprocal(out=recb[dr, :], in_=ps[ob][dr, :]),
                             reads=[psb[ob]], writes=[recbb])
                        S.op("dve", lambda e, ob=ob, dr=dr, vr=vr: e.tensor_tensor(
                            out=tt[vr, :], in0=ps[ob][vr, :], in1=recb[dr, :], op=ALU.mult),
                            reads=[psb[ob], recbb], writes=[ttb])
                        S.op("dve", lambda e, vr=vr, J=J, hp=hp: e.tensor_tensor(
                            out=mixA[vr, hp, 512 * J:512 * (J + 1)], in0=tt[vr, :], in1=sg[k][vr, 512 * J:512 * (J + 1)],
                            op=ALU.mult),
                            reads=[ttb, sgb[k]], writes_nc=[aTb])
            while ins_i < len(inserts):
                inserts[ins_i]()
                ins_i += 1
            while post:
                post.pop(0)()

        ogroup = [0]
        NPAIR = 8
        with ExitStack() as p1:
            wf = sb(p1, "wf", [128, 8, 16], BF16)
            lsp = sb(p1, "lsp", [16, L], F32)
            Sc = sb(p1, "Sc", [16, L], F32)
            et = [sb(p1, f"et{i}", [16, 512], F32) for i in range(2)]
            etb = S.bufs_n(2)
            Tt = sb(p1, "Tt", [16, 32], F32)
            TT = sb(p1, "TT", [32, 16], F32)
            Dsb = sb(p1, "Dsb", [16, 32], F32)
            wfb, lspb, Scb, Ttb, TTb, Dsbb, nctb = S.bufs_n(7)
            sig8 = sb(p1, "sig8", [16, NOWN], BF16)
            sig8b = S.buf()
            S.dma("pool", "wf", wf[:], w_in[:, C_F:C_F + 16].rearrange("(c p) n -> p c n", p=128), writes=[wfb])
            ntile = (L + 511) // 512
            for t in range(ntile):
                c0 = 512 * t
                N = min(512, L - c0)
                bank = 6 + t % 2
                k = t % 2
                for c in range(8):
                    S.op("pe", lambda e, bank=bank, c=c, c0=c0, N=N: e.matmul(
                        ps[bank][0:16, 0:N], lhsT=wf[:, c, :], rhs=xnT[:, c, c0:c0 + N], start=(c == 0), stop=(c == 7)),
                        reads=[wfb], writes=[psb[bank]], sig=(c == 7))
                S.op("act", lambda e, bank=bank, k=k, N=N: e.activation(
                    out=et[k][:, 0:N], in_=ps[bank][0:16, 0:N], func=AF.Exp, bias=nbf[:, 0:1], scale=-1.0),
                    reads=[psb[bank]], writes=[etb[k]])
                S.op("act", lambda e, k=k, c0=c0, N=N: e.activation(
                    out=lsp[:, c0:c0 + N], in_=et[k][:, 0:N], func=AF.Ln, bias=oneT[0:16, 0:1], scale=1.0),
                    reads=[etb[k]], writes_nc=[lspb])
            S.op("dve", lambda e: e.tensor_tensor_scan(
                out=Sc[:, :], data0=oneT[0:16, 0:1].to_broadcast([16, L]), data1=lsp[:, :], initial=0.0,
                op0=ALU.mult, op1=ALU.add), reads=[lspb], writes=[Scb])
            S.op("dve", lambda e: e.tensor_tensor(
                out=Tt[:, :], in0=Sc[:, 143:143 + 128 * 31 + 1:128], in1=Sc[:, 15:15 + 128 * 31 + 1:128], op=ALU.subtract),
                reads=[Scb], writes=[Ttb])
            S.op("pe", lambda e: e.transpose(out=ps[2][0:32, 0:16], in_=Tt[:, :], identity=ident_f[0:16, 0:16]),
                 reads=[Ttb], writes=[psb[2]])
            S.op("dve", lambda e: e.tensor_copy(out=TT[:, :], in_=ps[2][0:32, 0:16]), reads=[psb[2]], writes=[TTb])
            S.op("pe", lambda e: e.matmul(ps[3][0:16, 0:32], lhsT=TT[:, :], rhs=mmat[:, :], start=True, stop=True),
                 reads=[TTb], writes=[psb[3]])
            S.op("dve", lambda e: e.tensor_copy(out=Dsb[:, :], in_=ps[3][0:16, 0:32]), reads=[psb[3]], writes=[Dsbb])
            S.op("dve", lambda e: e.tensor_tensor(
                out=Sc[:, NM:L].rearrange("p (s t) -> p s t", t=128), in0=Sc[:, NM:L].rearrange("p (s t) -> p s t", t=128),
                in1=Dsb[:, :].unsqueeze(2).to_broadcast([16, 32, 128]), op=ALU.add),
                reads=[Scb, Dsbb], writes=[Scb])
            S.op("pe", lambda e: e.transpose(out=ps[4][0:16, 0:16], in_=Sc[:, 0:NM], identity=ident_f[0:16, 0:16]),
                 reads=[Scb], writes=[psb[4]])
            S.op("dve", lambda e: e.tensor_copy(out=negc_tok[0:16, 0, :], in_=ps[4][0:16, 0:16]),
                 reads=[psb[4]], writes_nc=[nctb])
            for s in range(32):
                S.op("pe", lambda e, s=s: e.transpose(out=ps[5][:, 16 * s:16 * s + 16],
                                                      in_=Sc[:, NM + 128 * s:NM + 128 * (s + 1)],
                                                      identity=ident_f[0:16, 0:16]),
                     reads=[Scb], writes=[psb[5]], sig=(s == 31))
            S.op("dve", lambda e: e.tensor_copy(out=negc_tok[:, 1:33, :].rearrange("p s h -> p (s h)"), in_=ps[5][:, :]),
                 reads=[psb[5]], writes_nc=[nctb])
            S.op("dve", lambda e: e.tensor_scalar(out=sig8[:, :], in0=Sc[:, NM:NM + NOWN], scalar1=-8.0, scalar2=None,
                                                  op0=ALU.mult), reads=[Scb], writes=[sig8b])
            for h_ in range(16):
                S.dma("sp", "sig8r", sig8r[8 * h_:8 * h_ + 8, :], sig8[h_:h_ + 1, :].rearrange("o (a b) -> o a b", b=256),
                      reads=[sig8b], writes_nc=[sig8rb])
            for f_ in proj_chunks(0):
                f_()
            g_proj(0)
            if NPAIR > 1:
                load_kqv(1)
            S.barrier()
            S.cps.pop()
            dump("negc_tok", negc_tok[:].rearrange("p s h -> p (s h)"), [128, 33 * 16], F32)
            dump("sig8", sig8[:], [16, NOWN], BF16)
            dump("xnT0", xnT[:, 0, :], [128, L], BF16)
            dump("xnT7", xnT[:, 7, :], [128, L], BF16)
            dump("xnh3", xnh[:, 3, :], [128, NHALO], BF16)
            dump("Sc", Sc[:], [16, L], F32)
            S.barrier()


        KA[1] = sb(pa, "KA1", [65, L], BF16)
        KB[1] = sb(pa, "KB1", [65, L], BF16)
        QA[1] = sb(pa, "QA1", [65, NOWN], BF16)
        QB[1] = sb(pa, "QB1", [65, NOWN], BF16)
        V1[1] = sb(pa, "V1_1", [128, 33, 192], BF16)
        sg[1] = sb(pa, "sg1", [128, NOWN], BF16)
        memset_aug(1)
        for hp in range(NPAIR):
            if hp > 0:
                g_proj(hp)
            if hp + 1 < NPAIR:
                pre = [lambda hp=hp: load_g(hp + 1)]
                if 1 <= hp <= 4:
                    def wobf_cast(hp=hp):
                        for q8 in (2 * (hp - 1), 2 * (hp - 1) + 1):
                            S.dma("pool", "wobf", wo_bf[256 * q8:256 * (q8 + 1), :], w_out[256 * q8:256 * (q8 + 1), :])
                    pre.append(wobf_cast)
                attention(hp, pre, proj_chunks(hp + 1), [lambda hp=hp: load_kqv(hp + 2)] if hp + 2 < NPAIR else [])
            else:
                def pool_w_prefetch():
                    S.dma("pool", "wu0", wq[:, 0, :, :],
                          w_in[:, C_U + 128 * 6:C_U + 128 * 7].rearrange("(c p) n -> p c n", p=128), writes_nc=[wkqvb, wgb])
                    S.dma("pool", "wu0", wq[:, 1, :, :],
                          w_in[:, C_GP + 128 * 6:C_GP + 128 * 7].rearrange("(c p) n -> p c n", p=128), writes_nc=[wkqvb, wgb])
                    S.dma("pool", "wpl", wq[:, 2:4, :, :].rearrange("p a c n -> p (a c n)").rearrange("p (g c e) -> p g c e", g=4, c=2),
                          w_pool.rearrange("g (c p) e -> p g c e", p=128), writes_nc=[wkqvb, wgb])
                attention(hp, [pool_w_prefetch], [])
        S.barrier()
        S.cps.pop()
        dump("KA", KA[(NPAIR - 1) % 2][:], [65, L], BF16)
        dump("KB", KB[(NPAIR - 1) % 2][:], [65, L], BF16)
        dump("QA", QA[(NPAIR - 1) % 2][:], [65, NOWN], BF16)
        dump("QB", QB[(NPAIR - 1) % 2][:], [65, NOWN], BF16)
        dump("V1", V1[(NPAIR - 1) % 2][:].rearrange("p s c -> p (s c)"), [128, 33 * 192], BF16)
        dump("sg", sg[(NPAIR - 1) % 2][:], [128, NOWN], BF16)
        dump("mixA", mixA[:].rearrange("p a t -> p (a t)"), [128, 8 * NOWN], BF16)
        S.barrier()

    es_p = ExitStack()
    mixP = sb(es_p, "mixP", [128, 8, NOWN], BF16)
    with ExitStack() as pp:
        wu = [wq[:, 0:2, :, :], sb(pp, "wu1", [128, 2, 8, 128], BF16)]
        wub = S.bufs_n(2)
        wpl = wq[:, 2:4, :, :].rearrange("p a c n -> p (a c n)").rearrange("p (g c e) -> p g c e", g=4, c=2)
        wplb = S.buf()
        U = [sb(pp, f"U{i}", [128, 8, 144], F32) for i in range(2)]
        Ub = S.bufs_n(2)
        T1 = sb(pp, "T1", [128, 8, 144], F32)
        T2 = sb(pp, "T2", [128, 8, 144], F32)
        T1b, T2b = S.bufs_n(2)
        dT = sb(pp, "dT", [128, 2, NOWN], BF16)
        dTb = S.bufs_n(2)
        sgp = [sb(pp, f"sgp{i}", [128, NOWN], BF16) for i in range(2)]
        sgpb = S.bufs_n(2)
        pTb = S.buf()

        def load_wu(ct):
            k = ct % 2
            S.dma("pool", f"wu{k}", wu[k][:, 0, :, :],
                  w_in[:, C_U + 128 * ct:C_U + 128 * (ct + 1)].rearrange("(c p) n -> p c n", p=128), writes_nc=[wub[k]])
            S.dma("pool", f"wu{k}", wu[k][:, 1, :, :],
                  w_in[:, C_GP + 128 * ct:C_GP + 128 * (ct + 1)].rearrange("(c p) n -> p c n", p=128), writes_nc=[wub[k]])

        pc = [0]

        def pbank():
            b = pc[0] % 4
            pc[0] += 1
            return b

        hcount = 0
        pending = []
        xnTb = S.buf()
        es_w = ExitStack()
        wo_box = []
        CT_ORDER = [6, 7, 4, 5, 2, 3, 0, 1]
        for cti, ct in enumerate(CT_ORDER):
            k = ct % 2
            gi = ct // 2
            nlev = gi + 1
            w = 2 ** nlev
            if cti + 1 < 8:
                load_wu(CT_ORDER[cti + 1])
            W = wu[k]
            for hf in range(2):
                uk = hcount % 2
                hcount += 1
                Uh, Uhb = U[uk], Ub[uk]
                for t in (2 * hf, 2 * hf + 1):
                    c0 = NM + 512 * t
                    bank = pbank()
                    for c in range(8):
                        S.op("pe", lambda e, bank=bank, c=c, c0=c0, W=W: e.matmul(
                            ps[bank][:, :], lhsT=W[:, 0, c, :], rhs=xnT[:, c, c0:c0 + 512], start=(c == 0), stop=(c == 7)),
                            reads=[wub[k], xnTb], writes=[psb[bank]], sig=(c == 7))
                    b0 = 4 * (t - 2 * hf)
                    S.op("dve", lambda e, bank=bank, b0=b0, Uh=Uh: e.tensor_copy(
                        out=Uh[:, b0:b0 + 4, 16:144], in_=ps[bank][:, :].rearrange("p (s c) -> p s c", c=128)),
                        reads=[psb[bank]], writes_nc=[Uhb])
                bank = pbank()
                for c in range(8):
                    S.op("pe", lambda e, bank=bank, c=c, W=W, hf=hf: e.matmul(
                        ps[bank][:, 0:128], lhsT=W[:, 0, c, :], rhs=xnh[:, c, 128 * hf:128 * (hf + 1)], start=(c == 0), stop=(c == 7)),
                        reads=[wub[k], xnTb], writes=[psb[bank]], sig=(c == 7))
                S.op("dve", lambda e, bank=bank, Uh=Uh: e.tensor_copy(
                    out=Uh[:, :, 0:16], in_=ps[bank][:, 0:128].rearrange("p (s c) -> p s c", c=16)),
                    reads=[psb[bank]], writes_nc=[Uhb])
                if hf == 0 and pending:
                    pending.pop(0)()
                for t in (2 * hf, 2 * hf + 1):
                    c0 = NM + 512 * t
                    bank = pbank()
                    for c in range(8):
                        S.op("pe", lambda e, bank=bank, c=c, c0=c0, W=W: e.matmul(
                            ps[bank][:, :], lhsT=W[:, 1, c, :], rhs=xnT[:, c, c0:c0 + 512], start=(c == 0), stop=(c == 7)),
                            reads=[wub[k], xnTb], writes=[psb[bank]], sig=(c == 7))
                    S.op("act", lambda e, bank=bank, t=t, k=k: e.activation(out=sgp[k][:, 512 * t:512 * (t + 1)], in_=ps[bank][:, :], func=AF.Silu),
                         reads=[psb[bank]], writes_nc=[sgpb[k]])
                src, srcb = Uh, Uhb
                tmp = [(T1, T1b), (T2, T2b)]
                for lv in range(nlev):
                    sh = 2 ** lv
                    lo = 16 - (w - 2 * sh)
                    dst, dstb = tmp[lv % 2]
                    weng = "dve"
                    S.op(weng, lambda e, src=src, dst=dst, sh=sh, lo=lo: e.tensor_tensor(
                        out=dst[:, :, lo:144], in0=src[:, :, lo:144], in1=src[:, :, lo - sh:144 - sh], op=ALU.add),
                        reads=[srcb], writes_nc=[dstb])
                    src, srcb = dst, dstb
                S.op("dve", lambda e, src=src, k=k, w=w, hf=hf, Uh=Uh: e.scalar_tensor_tensor(
                    out=dT[:, k, 1024 * hf:1024 * (hf + 1)].rearrange("p (s c) -> p s c", c=128), in0=src[:, :, 16:144], scalar=1.0 / w,
                    in1=Uh[:, :, 16:144], op0=ALU.mult, op1=ALU.subtract),
                    reads=[srcb, Uhb], writes_nc=[dTb[k]])
                if cti == 7 and hf == 1:
                    es_x.close()
                    wo_ = sb(es_w, "wo", [128, 16, D], BF16, side="right")
                    wobs_ = S.bufs_n(4)
                    wo_box.append((wo_, wobs_))
                    for q4 in range(4):
                        S.dma("sp", f"wo{q4}", wo_[:, 4 * q4:4 * q4 + 4, :],
                              wo_bf[512 * q4:512 * (q4 + 1), :].rearrange("(c p) n -> p c n", p=128),
                              writes_nc=[xnTb, wobs_[q4]])
            if k == 1:
                def wpool_emit(gi=gi):
                    for et_ in range(2):
                        ctp = 2 * gi + et_
                        for t in range(4):
                            bank = 4 + (pc[0] % 4)
                            pc[0] += 1
                            for cc in range(2):
                                S.op("pe", lambda e, bank=bank, cc=cc, gi=gi, et_=et_, t=t: e.matmul(
                                    ps[bank][:, :], lhsT=wpl[:, gi, cc, 128 * et_:128 * (et_ + 1)],
                                    rhs=dT[:, cc, 512 * t:512 * (t + 1)], start=(cc == 0), stop=(cc == 1)),
                                    reads=[wplb, dTb[0], dTb[1]], writes=[psb[bank]], sig=(cc == 1))
                            S.op("dve", lambda e, bank=bank, ctp=ctp, et_=et_, t=t: e.scalar_tensor_tensor(
                                out=mixP[:, ctp, 512 * t:512 * (t + 1)], in0=ps[bank][:, :], scalar=cv[:, 16 + ctp:17 + ctp],
                                in1=sgp[et_][:, 512 * t:512 * (t + 1)], op0=ALU.mult, op1=ALU.mult),
                                reads=[psb[bank], sgpb[et_]], writes_nc=[pTb])
                pending.append(wpool_emit)
        while pending:
            pending.pop(0)()
        S.barrier(skip_slots=("wo0", "wo1", "wo2", "wo3"), keep_bufs=wo_box[0][1])
    wo, wobs = wo_box[0]

    with ExitStack() as po:
        lnb = sb(po, "lnb", [128, 4, D], F32)
        lnbb = S.buf()
        NX = 3
        o_xt = [sb(po, f"oxt{i}", [128, D], F32) for i in range(NX)]
        o_xtb = S.bufs_n(NX)
        x0 = [sb(po, f"x0{i}", [128, D], F32) for i in range(NX)]
        x0b = S.bufs_n(NX)
        zt = [sb(po, f"zt{i}", [128, D], F32) for i in range(2)]
        ztb = S.bufs_n(2)
        ot = [sb(po, f"ot{i}", [128, D], F32) for i in range(2)]
        otb = S.bufs_n(2)
        NS = 6
        o_st = [sb(po, f"ost{i}", [128, 2, 6], F32) for i in range(NS)]
        o_mv = [sb(po, f"omv{i}", [128, 2], F32) for i in range(NS)]
        o_sd = [sb(po, f"osd{i}", [128, 1], F32) for i in range(NS)]
        o_rs = [sb(po, f"ors{i}", [128, 1], F32) for i in range(NS)]
        o_stb, o_mvb, o_sdb, o_rsb = S.bufs_n(NS), S.bufs_n(NS), S.bufs_n(NS), S.bufs_n(NS)
        for i4 in range(4):
            S.dma("sp", "lnb", lnb[:, i4, :], lnrows_d[i4:i4 + 1, :].partition_broadcast(128)[:, 0, :], writes_nc=[lnbb])
        S.op("pool", lambda e: e.tensor_scalar(out=lnb[:, 0:2, :], in0=lnb[:, 0:2, :], scalar1=ALPHA, scalar2=None, op0=ALU.mult),
             reads=[lnbb], writes=[lnbb])

        def ln_stats(kk, src):
            S.op("dve", lambda e: e.bn_stats(out=o_st[kk][:, 0, :], in_=src[0][:, 0:512]), reads=[src[1]], writes=[o_stb[kk]])
            S.op("dve", lambda e: e.bn_stats(out=o_st[kk][:, 1, :], in_=src[0][:, 512:1024]), reads=[src[1]], writes_nc=[o_stb[kk]])
            S.op("dve", lambda e: e.bn_aggr(out=o_mv[kk][:, :], in_=o_st[kk][:].rearrange("p a b -> p (a b)")),
                 reads=[o_stb[kk]], writes=[o_mvb[kk]])
            S.op("act", lambda e: e.activation(out=o_sd[kk][:, :], in_=o_mv[kk][:, 1:2], func=AF.Sqrt, bias=epsT[:, 0:1], scale=1.0),
                 reads=[o_mvb[kk]], writes=[o_sdb[kk]])

        def ln_stats_b(kk):
            S.op("dve", lambda e: e.reciprocal(out=o_rs[kk][:, :], in_=o_sd[kk][:, :]), reads=[o_sdb[kk]], writes=[o_rsb[kk]])

        def prep(i):
            k = i % NX
            kk = i % 3
            S.dma("sp", f"oxt{k}", o_xt[k][:, :], xs[NM + 128 * i:NM + 128 * (i + 1), :], writes=[o_xtb[k]])
            S.op("dve", lambda e, k=k, i=i: e.tensor_scalar(
                out=x0[k][:, :], in0=o_xt[k][:, :], scalar1=mv_all[:, 1 + i, 0:1], scalar2=rs_all[:, 1 + i:2 + i],
                op0=ALU.subtract, op1=ALU.mult), reads=[o_xtb[k]], writes=[x0b[k]])
            S.op("pool", lambda e, k=k: e.tensor_tensor(out=x0[k][:, :], in0=x0[k][:, :], in1=lnb[:, 0, :], op=ALU.mult),
                 reads=[x0b[k], lnbb], writes=[x0b[k]])
            S.op("pool", lambda e, k=k: e.tensor_tensor(out=x0[k][:, :], in0=x0[k][:, :], in1=lnb[:, 1, :], op=ALU.add),
                 reads=[x0b[k], lnbb], writes=[x0b[k]])

        prep(0)
        prep(1)
        for i in range(16):
            k = i % NX
            k2 = i % 2
            for half in range(2):
                bank = (2 * i + half) % 8
                for c in range(16):
                    S.op("pe", lambda e, bank=bank, c=c, i=i, half=half: e.matmul(
                        ps[bank][:, :], lhsT=(mixA if c < 8 else mixP)[:, c % 8, 128 * i:128 * (i + 1)],
                        rhs=wo[:, c, 512 * half:512 * (half + 1)],
                        start=(c == 0), stop=(c == 15)), reads=[wobs[c // 4]], writes=[psb[bank]], sig=(c == 15))
                S.op("dve", lambda e, bank=bank, k=k, k2=k2, half=half: e.tensor_tensor(
                    out=zt[k2][:, 512 * half:512 * (half + 1)], in0=ps[bank][:, :], in1=x0[k][:, 512 * half:512 * (half + 1)],
                    op=ALU.add), reads=[psb[bank], x0b[k]], writes_nc=[ztb[k2]])
            kk = 3 + (i % 3)
            ln_stats(kk, (zt[k2], ztb[k2]))
            if i + 2 < 16:
                prep(i + 2)
            ln_stats_b(kk)
            S.op("dve", lambda e, k2=k2, kk=kk: e.scalar_tensor_tensor(
                out=ot[k2][:, :], in0=zt[k2][:, :], scalar=o_mv[kk][:, 0:1], in1=lnb[:, 2, :],
                op0=ALU.subtract, op1=ALU.mult), reads=[ztb[k2], o_mvb[kk], lnbb], writes=[otb[k2]])
            S.op("dve", lambda e, k2=k2, kk=kk: e.scalar_tensor_tensor(
                out=ot[k2][:, :], in0=ot[k2][:, :], scalar=o_rs[kk][:, 0:1], in1=lnb[:, 3, :],
                op0=ALU.mult, op1=ALU.add), reads=[otb[k2], o_rsb[kk], lnbb], writes=[otb[k2]])
            S.dma("act", f"oy{k2}", y[128 * i:128 * (i + 1), :], ot[k2][:, :], reads=[otb[k2]])
        S.barrier()

    with ExitStack() as semstack:
        sems = {}
        for e in Sched.ENG:
            sems[("E", e)] = semstack.enter_context(nc.semaphore(f"sem_{e}"))
        for slot in S.dcount:
            sems[("D", slot)] = semstack.enter_context(nc.semaphore(f"semd_{slot}"))
        with nc.Block() as block:
            S.replay(nc, block, sems, upto)
    es_w.close()
    es_p.close()
    es.close()
    return nc


def _core_inputs(c, x, meta_tokens, shared):
    b, r = c // 2, c % 2
    hfull = np.concatenate([meta_tokens, x[b]], axis=0)
    own_g = [2 * i + r for i in range(16)]
    oth_g = [2 * j + 1 - r for j in range(16)]
    xb = x[b].reshape(32, 128, D)
    halo = np.stack([hfull[128 * g:128 * g + 16] for g in own_g], axis=0).reshape(NHALO, D)
    xs = np.concatenate([meta_tokens, xb[own_g].reshape(-1, D), xb[oth_g].reshape(-1, D), halo], axis=0)
    gl = np.array(own_g + oth_g)
    sl = np.arange(32)
    mm = (gl[:, None] < gl[None, :]).astype(np.float32) - (sl[:, None] < sl[None, :]).astype(np.float32)
    kq = np.arange(128)
    tri = np.where(kq[:, None] <= kq[None, :], 0.0, MASKV).astype(np.float32)
    mx = np.full((128, 128), MASKV if r == 0 else 0.0, np.float32)
    d = dict(shared)
    d["xs"] = np.ascontiguousarray(xs, dtype=np.float32)
    d["mmat"] = np.ascontiguousarray(mm)
    d["masks"] = np.ascontiguousarray(np.concatenate([tri, mx], axis=1))
    return d, own_g


def kernel(x, meta_tokens, ln_in_g, ln_in_b, w_in, b_forget, w_pool, pool_scale, w_out, ln_g, ln_b):
    x = np.asarray(x, np.float32)
    meta_tokens = np.asarray(meta_tokens, np.float32)
    f = lambda a: np.ascontiguousarray(np.asarray(a, np.float32))
    cvec = np.concatenate([f(ln_in_g).reshape(8, 128).T, f(ln_in_b).reshape(8, 128).T,
                           f(pool_scale)[0].reshape(8, 128).T], axis=1)
    shared = {
        "w_in": f(w_in)[0], "w_pool": f(w_pool)[0], "w_out": f(w_out)[0],
        "cvec": np.ascontiguousarray(cvec), "bfv": f(b_forget)[0].reshape(16, 1),
        "ident": np.eye(128, dtype=np.float32),
        "lnrows": np.ascontiguousarray(np.stack([f(ln_in_g), f(ln_in_b), f(ln_g)[0], f(ln_b)[0]], axis=0)),
    }
    in_maps, owns = [], []
    for c in range(8):
        d, own_g = _core_inputs(c, x, meta_tokens, shared)
        in_maps.append(d)
        owns.append(own_g)
    nc = build_nc()
    res = run_bass_kernel_spmd(nc, in_maps, core_ids=list(range(8)))
    out = np.empty((4, SEQ, D), np.float32)
    for c in range(8):
        yb = np.asarray(res.results[c]["y"], np.float32).reshape(16, 128, D)
        ov = out[c // 2].reshape(32, 128, D)
        ov[owns[c]] = yb
    return out
```
